# Optimizing a Trainium2 kernel written in Bass

```python
import jax, jax.numpy as jnp
from jax import lax
import numpy as np

D_MODEL = 2048
BATCH = 2
SEQ = 4096
DEPTH = 1

D_MIX = D_MODEL
GMLP_WIDTH = D_MIX // 2
GMLP_GROUP = 128
GMLP_HEADS = GMLP_WIDTH // GMLP_GROUP
CHUNK = 128
V_HEAD = 128
MLA_WIDTH = D_MIX - GMLP_WIDTH
MLA_HEADS = MLA_WIDTH // V_HEAD
QK_NOPE = 128
QK_ROPE = 64
Q_RANK = D_MODEL // 4
KV_RANK = D_MODEL // 8
IN_COLS = 2 * GMLP_WIDTH + Q_RANK + KV_RANK + QK_ROPE
D_FF = ((8 * D_MODEL // 3 + 255) // 256) * 256
Q_BLOCK = 128
ROPE_THETA = 10000.0
EPS = 1e-6
N_MOD = 9

kernel_name = "hybrid_gmlp_mla_macaron_adaln_encoder"


def _rms(x, g):
    xf = x.astype(jnp.float32)
    y = xf * lax.rsqrt(jnp.mean(xf * xf, axis=-1, keepdims=True) + EPS)
    return (y * g.astype(jnp.float32)).astype(x.dtype)


def _modulate(x, shift, scale):
    return x * (1 + scale[:, None, :]) + shift[:, None, :]


def _swiglu(x, w_gate, w_up, w_down):
    return (jax.nn.silu(x @ w_gate) * (x @ w_up)) @ w_down


def _rotate(x, cos, sin):
    x1, x2 = jnp.split(x, 2, axis=-1)
    return jnp.concatenate([x1 * cos - x2 * sin, x1 * sin + x2 * cos], axis=-1)


def _gmlp_mixer(u, v, v_norm, w_s, b_s):
    B, S, _ = u.shape
    u = jax.nn.gelu(u)
    v = _rms(jax.nn.gelu(v), v_norm)
    v = v.reshape(B, S // CHUNK, CHUNK, GMLP_HEADS, GMLP_GROUP)
    mixed = jnp.einsum('hpq,bnqhc->bnphc', w_s, v) + b_s.T[None, None, :, :, None]
    return u * mixed.reshape(B, S, GMLP_WIDTH)


def _mla_mixer(q_lat, kv_lat, k_pe, cos, sin, q_lat_norm, w_uq, kv_lat_norm, w_ukv,
               q_nope_norm, q_rope_norm, k_nope_norm, k_rope_norm):
    B, S, _ = q_lat.shape
    q = (_rms(q_lat, q_lat_norm) @ w_uq).reshape(B, S, MLA_HEADS, QK_NOPE + QK_ROPE)
    q_nope = _rms(q[..., :QK_NOPE], q_nope_norm)
    q_pe = _rotate(_rms(q[..., QK_NOPE:], q_rope_norm),
                   cos[:, :, None, :], sin[:, :, None, :])
    kv = (_rms(kv_lat, kv_lat_norm) @ w_ukv).reshape(B, S, MLA_HEADS, QK_NOPE + V_HEAD)
    k_nope = _rms(kv[..., :QK_NOPE], k_nope_norm)
    v = kv[..., QK_NOPE:]
    k_pe = _rotate(_rms(k_pe, k_rope_norm), cos, sin)
    scale = (QK_NOPE + QK_ROPE) ** -0.5
    nb = S // Q_BLOCK
    qn_b = q_nope.reshape(B, nb, Q_BLOCK, MLA_HEADS, QK_NOPE).transpose(1, 0, 2, 3, 4)
    qp_b = q_pe.reshape(B, nb, Q_BLOCK, MLA_HEADS, QK_ROPE).transpose(1, 0, 2, 3, 4)

    def block(args):
        qn, qp = args
        s = (jnp.einsum('bqhd,bkhd->bhqk', qn, k_nope)
             + jnp.einsum('bqhd,bkd->bhqk', qp, k_pe))
        p = jax.nn.softmax(s.astype(jnp.float32) * scale, axis=-1).astype(v.dtype)
        return jnp.einsum('bhqk,bkhd->bqhd', p, v)

    o = lax.map(block, (qn_b, qp_b))
    return o.transpose(1, 0, 2, 3, 4).reshape(B, S, MLA_WIDTH)


def setup_inputs(seed: int = 0) -> dict:
    key = jax.random.key(seed)
    ks = iter(jax.random.split(key, 40))

    def nrm(shape, scale):
        return jax.random.normal(next(ks), shape, jnp.float32) * scale

    def gain(shape):
        return 1.0 + 0.02 * jax.random.normal(next(ks), shape, jnp.float32)

    L, D = DEPTH, D_MODEL
    positions = jnp.tile(jnp.arange(SEQ, dtype=jnp.int32)[None, :], (BATCH, 1))
    return {
        "x": nrm((BATCH, SEQ, D), 1.0),
        "c": nrm((BATCH, D), 1.0),
        "positions": positions,
        "w_ada": nrm((L, D, N_MOD * D), 0.5 * D ** -0.5),
        "b_ada": nrm((L, N_MOD * D), 0.01),
        "ffn1_norm": gain((L, D)),
        "ffn1_w_gate": nrm((L, D, D_FF), D ** -0.5),
        "ffn1_w_up": nrm((L, D, D_FF), D ** -0.5),
        "ffn1_w_down": nrm((L, D_FF, D), D_FF ** -0.5),
        "mix_norm": gain((L, D)),
        "w_in": nrm((L, D, IN_COLS), D ** -0.5),
        "gmlp_v_norm": gain((L, GMLP_WIDTH)),
        "gmlp_w_s": nrm((L, GMLP_HEADS, CHUNK, CHUNK), CHUNK ** -0.5),
        "gmlp_b_s": gain((L, GMLP_HEADS, CHUNK)),
        "q_lat_norm": gain((L, Q_RANK)),
        "w_uq": nrm((L, Q_RANK, MLA_HEADS * (QK_NOPE + QK_ROPE)), Q_RANK ** -0.5),
        "kv_lat_norm": gain((L, KV_RANK)),
        "w_ukv": nrm((L, KV_RANK, MLA_HEADS * (QK_NOPE + V_HEAD)), KV_RANK ** -0.5),
        "q_nope_norm": gain((L, QK_NOPE)),
        "q_rope_norm": gain((L, QK_ROPE)),
        "k_nope_norm": gain((L, QK_NOPE)),
        "k_rope_norm": gain((L, QK_ROPE)),
        "out_norm_gmlp": gain((L, GMLP_WIDTH)),
        "out_norm_mla": gain((L, MLA_WIDTH)),
        "w_out": nrm((L, D_MIX, D), D_MIX ** -0.5),
        "ffn2_norm": gain((L, D)),
        "ffn2_w_gate": nrm((L, D, D_FF), D ** -0.5),
        "ffn2_w_up": nrm((L, D, D_FF), D ** -0.5),
        "ffn2_w_down": nrm((L, D_FF, D), D_FF ** -0.5),
        "final_norm": gain((L, D)),
    }


def reference(x, c, positions, w_ada, b_ada, ffn1_norm, ffn1_w_gate, ffn1_w_up, ffn1_w_down,
              mix_norm, w_in, gmlp_v_norm, gmlp_w_s, gmlp_b_s, q_lat_norm, w_uq,
              kv_lat_norm, w_ukv, q_nope_norm, q_rope_norm, k_nope_norm, k_rope_norm,
              out_norm_gmlp, out_norm_mla, w_out, ffn2_norm, ffn2_w_gate, ffn2_w_up,
              ffn2_w_down, final_norm):
    inv_freq = ROPE_THETA ** (-jnp.arange(0, QK_ROPE, 2, dtype=jnp.float32) / QK_ROPE)
    ang = positions.astype(jnp.float32)[..., None] * inv_freq
    cos = jnp.cos(ang).astype(x.dtype)
    sin = jnp.sin(ang).astype(x.dtype)
    c_act = jax.nn.silu(c)
    o1 = 2 * GMLP_WIDTH
    o2 = o1 + Q_RANK
    o3 = o2 + KV_RANK

    h = x
    for l in range(DEPTH):
        mod = c_act @ w_ada[l] + b_ada[l]
        sh1, sc1, g1, sh2, sc2, g2, sh3, sc3, g3 = jnp.split(mod, N_MOD, axis=-1)

        n = _modulate(_rms(h, ffn1_norm[l]), sh1, sc1)
        h = h + 0.5 * g1[:, None, :] * _swiglu(n, ffn1_w_gate[l], ffn1_w_up[l], ffn1_w_down[l])

        n = _modulate(_rms(h, mix_norm[l]), sh2, sc2)
        proj = n @ w_in[l]
        a = _gmlp_mixer(proj[..., :GMLP_WIDTH], proj[..., GMLP_WIDTH:o1],
                        gmlp_v_norm[l], gmlp_w_s[l], gmlp_b_s[l])
        m = _mla_mixer(proj[..., o1:o2], proj[..., o2:o3], proj[..., o3:], cos, sin,
                       q_lat_norm[l], w_uq[l], kv_lat_norm[l], w_ukv[l],
                       q_nope_norm[l], q_rope_norm[l], k_nope_norm[l], k_rope_norm[l])
        merged = jnp.concatenate([_rms(a, out_norm_gmlp[l]), _rms(m, out_norm_mla[l])], axis=-1)
        h = h + g2[:, None, :] * (merged @ w_out[l])

        n = _modulate(_rms(h, ffn2_norm[l]), sh3, sc3)
        h = h + 0.5 * g3[:, None, :] * _swiglu(n, ffn2_w_gate[l], ffn2_w_up[l], ffn2_w_down[l])

        h = _rms(h, final_norm[l])
    return h
```

```python
import numpy as np
import ml_dtypes
from contextlib import ExitStack
import concourse.bass as bass
import concourse.mybir as mybir
from concourse.bass_utils import run_bass_kernel_spmd

F32 = mybir.dt.float32
BF16 = mybir.dt.bfloat16
I32 = mybir.dt.int32
AF = mybir.ActivationFunctionType
ALU = mybir.AluOpType
AX = mybir.AxisListType

D = 2048
T = 1024
NCH = 16
DFF = 5632
NBLK = 11
EPS = 1e-6
NS = 5
STAGE = 3
MIXCUT = 8
DBG_SKIP_FFN = False
DBGSUB = 9
DBGX = 0
DBGC = 9
DBGH1 = False
SAME_SYNC = True
SAME_RAW_ONLY = False
SM_SCALE = float(192 ** -0.5)


class Buf:
    __slots__ = ("name", "w", "rs", "frozen")

    def __init__(self, name):
        self.name = name
        self.w = None
        self.rs = []
        self.frozen = False


class DSem:
    def __init__(self, key):
        self.key = key
        self.n = 0


class Prog:
    ENG = ("pe", "act", "dve", "pool", "sp")

    def __init__(self):
        self.ops = {e: [] for e in self.ENG}
        self.cnt = {e: 0 for e in self.ENG}
        self.known = {e: {} for e in self.ENG}

    def _waits(self, eng, toks):
        out = []
        k = self.known[eng]
        for t in toks:
            if t is None:
                continue
            sem, val = t
            if sem == eng and (eng == "pe" or not SAME_SYNC):
                continue
            if k.get(sem, 0) >= val:
                continue
            k[sem] = val
            out.append((sem, val))
        return out

    def op(self, eng, fn, reads=(), writes=(), inc=True):
        toks = []
        for b in reads:
            toks.append(b.w)
        for b in writes:
            for t in [b.w] + b.rs:
                if SAME_RAW_ONLY and t is not None and t[0] == eng:
                    continue
                toks.append(t)
        ws = self._waits(eng, toks)
        if inc:
            self.cnt[eng] += 1
            tok = (eng, self.cnt[eng])
        else:
            tok = (eng, self.cnt[eng] + 1)
        self.ops[eng].append((ws, fn, (eng, 1) if inc else None))
        for b in reads:
            if not b.frozen:
                b.rs.append(tok)
        for b in writes:
            b.w = tok
            b.rs = []
        return tok

    def dma(self, eng, fn, dsem, reads=(), writes=(), extra=(), amt=16):
        toks = list(extra)
        for b in reads:
            toks.append(b.w)
        for b in writes:
            toks.append(b.w)
            toks.extend(b.rs)
        ws = self._waits(eng, toks)
        dsem.n += amt
        tok = (dsem.key, dsem.n)
        self.ops[eng].append((ws, fn, (dsem.key, amt)))
        for b in reads:
            if not b.frozen:
                b.rs.append(tok)
        for b in writes:
            b.w = tok
            b.rs = []
        return tok

    def wait_all(self, eng, toks):
        ws = self._waits(eng, toks)
        if ws:
            self.ops[eng].append((ws, None, None))


def inherit(new, olds):
    new.w = None
    rs = []
    for o in olds:
        if o.w is not None:
            rs.append(o.w)
        rs.extend(o.rs)
    new.rs = rs


def build_program():
    nc = bass.Bass("TRN2", target_bir_lowering=False)
    P = Prog()

    def din(name, shape, dt=F32):
        return nc.dram_tensor(name, list(shape), dt, kind="ExternalInput").ap()

    x_d = din("x", [T, D])
    c_d = din("c", [D])
    pos_d = din("positions", [T], I32)
    w_ada = din("w_ada", [D, 256 if DBG_SKIP_FFN else 9 * D])
    b_ada = din("b_ada", [9 * D])
    ffn_w = {}
    for k in (1, 2):
        if DBG_SKIP_FFN:
            ffn_w[k] = (din(f"ffn{k}_w_gate", [128, 256]), din(f"ffn{k}_w_up", [128, 256]), din(f"ffn{k}_w_down", [128, 256]))
        else:
            ffn_w[k] = (din(f"ffn{k}_w_gate", [D, DFF]), din(f"ffn{k}_w_up", [D, DFF]), din(f"ffn{k}_w_down", [DFF, D]))
    ffn1_norm = din("ffn1_norm", [D]); mix_norm = din("mix_norm", [D]); ffn2_norm = din("ffn2_norm", [D]); final_norm = din("final_norm", [D])
    w_in = din("w_in", [D, 2880])
    gmlp_v_norm = din("gmlp_v_norm", [1024]); gmlp_w_s = din("gmlp_w_s", [8, 128, 128]); gmlp_b_s = din("gmlp_b_s", [1024])
    q_lat_norm = din("q_lat_norm", [512]); w_uq = din("w_uq", [512, 1536])
    kv_lat_norm = din("kv_lat_norm", [256]); w_ukv = din("w_ukv", [256, 2048])
    q_nope_norm = din("q_nope_norm", [128]); q_rope_norm = din("q_rope_norm", [64])
    k_nope_norm = din("k_nope_norm", [128]); k_rope_norm = din("k_rope_norm", [64])
    out_norm_gmlp = din("out_norm_gmlp", [1024]); out_norm_mla = din("out_norm_mla", [1024])
    w_out = din("w_out", [D, D])
    ident_d = din("ident", [128, 128])
    invf_d = din("inv_freq", [32])
    y_d = nc.dram_tensor("y", [T, D], F32, kind="ExternalOutput").ap()
    hsp = nc.dram_tensor("hsp", [128, NCH * T], F32)
    gin = nc.dram_tensor("gin", [384, T], BF16)
    gout = nc.dram_tensor("gout", [4 * 384, T], BF16)

    es = ExitStack()

    def sb(name, shape, dt=F32):
        return es.enter_context(nc.sbuf_tensor(name, list(shape), dt))

    def ps(name, shape, dt=F32):
        return es.enter_context(nc.psum_tensor(name, list(shape), dt))

    sem_names = ["pe", "act", "dve", "pool"]
    dsem_keys = [f"ring{i}" for i in range(NS)] + ["par", "xs0", "xs1", "os0", "os1", "spill", "reload", "gin", "gout", "cc"]
    semh = {}
    for k in sem_names + dsem_keys:
        semh[k] = es.enter_context(nc.semaphore(k))
    dsem = {k: DSem(k) for k in dsem_keys}

    A = sb("A", [128, 32768], BF16)
    NT = sb("NT", [128, 16384], BF16)
    Bt = sb("Bt", [128, 8192], BF16)
    RING = sb("RING", [128, NS, 4096], BF16)
    OWN = sb("OWN", [128, 3, T], BF16)
    QLT = Bt[:, 4096:8192].rearrange("p (c t) -> p c t", c=4)
    PR = OWN[:, 0:2, :].rearrange("p a b -> p (a b)")[:, 0:1536].rearrange("p (a b) -> p a b", a=3)
    cols = sb("cols", [128, 104]); badac = sb("badac", [128, 144]); modc = sb("modc", [128, 144])
    der = sb("der", [128, 8, 16])
    cact = sb("cact", [128, 16], BF16)
    ident_f = sb("ident_f", [128, 128]); ident_b = sb("ident_b", [128, 128], BF16)
    ones_f = sb("ones_f", [128, 128]); ones_b = sb("ones_b", [128, 128], BF16)
    R1 = sb("R1", [128, 128]); R2 = sb("R2", [128, 128]); R3 = sb("R3", [16, 128]); PI = sb("PI", [8, 128], I32)
    wsT = sb("wsT", [128, 8, 128], BF16)
    bc1 = sb("bc1", [128, 1024])
    qlat_bc = sb("qlat_bc", [128, 512]); kvlat_bc = sb("kvlat_bc", [128, 256]); krope_bc = sb("krope_bc", [128, 64])
    qn_bc = sb("qn_bc", [128, 128]); qr_bc = sb("qr_bc", [128, 64]); kn_bc = sb("kn_bc", [128, 128]); invf_bc = sb("invf_bc", [128, 32])
    cosT = sb("cosT", [128, 8, 32]); sinT = sb("sinT", [128, 8, 32]); angT = sb("angT", [128, 8, 32])
    rstd_bc = sb("rstd_bc", [128, 2, 512])
    sqr = sb("sqr", [128, 2, 512], BF16); tmpr = sb("tmpr", [128, 2, 512])
    S1 = sb("S1", [128, 1536]); S2 = sb("S2", [128, 1024])
    st = sb("st", [128, 64])
    kpe2 = sb("kpe2", [128, 128], BF16)
    qnb = sb("qnb", [128, 8, 128], BF16); qpb = sb("qpb", [128, 8, 64], BF16)
    knb = sb("knb", [128, 2, 4, 128], BF16)

    PSB = [ps(f"P{i}", [128, 512]) for i in range(5)]
    PA = ps("PA", [128, 1024])
    TRB = ps("TRB", [128, 1024], BF16)
    pb = [Buf(f"P{i}") for i in range(5)]
    pa0, pa1 = Buf("PA0"), Buf("PA1")
    trb = Buf("TRB")

    hF = A.bitcast(F32)
    xsF = NT.bitcast(F32)

    def hv(c, half):
        return hF[:, c * T + half * 512: c * T + half * 512 + 512]

    def ntv(c, half):
        return NT[:, c * T + half * 512: c * T + half * 512 + 512]

    hb = [[Buf(f"h{c}_{h}") for h in range(2)] for c in range(NCH)]
    ntb = [[Buf(f"nt{c}_{h}") for h in range(2)] for c in range(NCH)]
    actb = [[Buf(f"act{f}_{h}") for h in range(2)] for f in range(4)]
    sqb = [Buf("sq0"), Buf("sq1")]
    tmpb = [Buf("tmp0"), Buf("tmp1")]
    rsb = [Buf("rs0"), Buf("rs1")]
    ring_b = [Buf(f"ring{i}") for i in range(NS)]
    parb = Buf("params")
    colsb = Buf("cols"); modb = Buf("modc"); derb = Buf("der"); cactb = Buf("cact"); onesb = Buf("ones"); identb = Buf("ident"); wsb = Buf("wsT"); trigb = Buf("trig"); angb = Buf("ang"); r1b = Buf("r1")
    qnbb, kpe2b, qpbb = Buf("qnb"), Buf("kpe2"), Buf("qpb")
    knbb = [Buf("knb0"), Buf("knb1")]
    S1b, S2b, stb = Buf("S1"), Buf("S2"), Buf("st")

    items = []
    state = {"issued": 0, "released": [False] * 4096}

    def add_item(src, a, b):
        items.append((src, a, b))
        return len(items) - 1

    def pump():
        while state["issued"] < len(items):
            k = state["issued"]
            if k >= NS and not state["released"][k - NS]:
                break
            src, a, b = items[k]
            s = k % NS
            dst = RING[:, s, 0:a * b].rearrange("p (a b) -> p a b", a=a)
            P.dma("pool", (lambda e, dst=dst, src=src: e.dma_start(out=dst, in_=src)), dsem[f"ring{s}"], writes=[ring_b[s]])
            state["issued"] += 1

    def get_item(k):
        pump()
        assert state["issued"] > k, f"ring item {k} not issued (ring too small)"
        src, a, b = items[k]
        s = k % NS
        return ring_b[s], RING[:, s, 0:a * b].rearrange("p (a b) -> p a b", a=a)

    def release(k):
        state["released"][k] = True
        pump()

    def colitem(w, c0, n):
        return add_item(w[:, c0:c0 + n].rearrange("(k p) n -> p k n", p=128), w.shape[0] // 128, n)

    def rowitem(w, r0, nrow):
        return add_item(w[r0:r0 + nrow, :].rearrange("(j p) n -> p j n", p=128), nrow // 128, w.shape[1])

    ada_items = [colitem(w_ada, 0 if DBG_SKIP_FFN else j * 256, 256) for j in range(16)]
    ffn_items = {1: [], 2: []}
    ada_rest = []

    def add_ffn_items(k):
        Wg, Wu, Wd = ffn_w[k]
        for b in range(NBLK):
            blk = {}
            for pr in range(2):
                if DBG_SKIP_FFN:
                    break
                f0 = (b * 4 + pr * 2) * 128
                blk[("g", pr)] = colitem(Wg, f0, 256)
                blk[("u", pr)] = colitem(Wu, f0, 256)
            if k == 1 and b == 0:
                blk["ada_early"] = [(j, colitem(w_ada, 0 if DBG_SKIP_FFN else j * 256, 256)) for j in range(16, 24)]
            for pr in range(2):
                if DBG_SKIP_FFN:
                    break
                f0 = (b * 4 + pr * 2) * 128
                blk[("d", pr)] = rowitem(Wd, f0, 256)
            ffn_items[k].append(blk)
            if k == 1:
                lo = 24 + (48 * b) // NBLK
                hi = 24 + (48 * (b + 1)) // NBLK
                blk["ada"] = [(j, colitem(w_ada, 0 if DBG_SKIP_FFN else j * 256, 256)) for j in range(lo, hi)]

    add_ffn_items(1)
    win_u = [colitem(w_in, i * 256, 256) for i in range(4)]
    win_v = [colitem(w_in, 1024 + i * 256, 256) for i in range(4)]
    win_cq = [colitem(w_in, 2048 + i * 256, 256) for i in range(2)]
    win_ckv = colitem(w_in, 2560, 256)
    win_kpe = colitem(w_in, 2624, 256)
    n_items_dbg = len(items)
    wuq_items = [colitem(w_uq, i * 768, 768) for i in range(2)]
    wukv_item = rowitem(w_ukv, 0, 256)
    wout_items = [colitem(w_out, i * 256, 256) for i in range(8)]
    add_ffn_items(2)

    if DBGX == 5:
        del items[n_items_dbg:]

    def mm(out, lhsT, rhs, start, stop, reads, writes, inc=None, skip=False):
        if inc is None:
            inc = stop
        if skip:
            P.op("pe", (lambda e: e.matmul(out, lhsT, rhs, start=start, stop=stop, skip_group_check=True)), reads=reads, writes=writes, inc=inc)
        else:
            P.op("pe", (lambda e: e.matmul(out, lhsT, rhs, start=start, stop=stop)), reads=reads, writes=writes, inc=inc)

    def tr(out, in_, ident, reads, writes, inc):
        P.op("pe", (lambda e: e.transpose(out, in_, ident)), reads=reads, writes=writes, inc=inc)

    def act(out, in_, func, reads, writes, bias=None, scale=None):
        kw = {}
        if bias is not None:
            kw["bias"] = bias
        if scale is not None:
            kw["scale"] = scale
        P.op("act", (lambda e: e.activation(out=out, in_=in_, func=func, **kw)), reads=reads, writes=writes)

    def dve(fn, reads, writes):
        P.op("dve", fn, reads=reads, writes=writes)

    def tt_(out, in0, in1, op, reads, writes):
        dve((lambda e: e.tensor_tensor(out=out, in0=in0, in1=in1, op=op)), reads, writes)

    def stt(out, in0, scalar, in1, op0, op1, reads, writes):
        dve((lambda e: e.scalar_tensor_tensor(out=out, in0=in0, scalar=scalar, in1=in1, op0=op0, op1=op1)), reads, writes)

    def ts(out, in0, s1, s2, op0, op1, reads, writes):
        if s2 is None:
            dve((lambda e: e.tensor_single_scalar(out=out, in_=in0, scalar=s1, op=op0)), reads, writes)
        else:
            dve((lambda e: e.tensor_scalar(out=out, in0=in0, scalar1=s1, scalar2=s2, op0=op0, op1=op1)), reads, writes)

    def vcopy(out, in_, reads, writes):
        dve((lambda e: e.tensor_copy(out=out, in_=in_)), reads, writes)

    def rsum(out, in_, reads, writes):
        dve((lambda e: e.tensor_reduce(out=out, in_=in_, axis=AX.X, op=ALU.add)), reads, writes)

    def recip(out, in_, reads, writes):
        dve((lambda e: e.reciprocal(out=out, in_=in_)), reads, writes)

    def rstd_small(dst, ss, n, reads, writes):
        act(dst, ss, AF.Sqrt, reads, writes, bias=EPS, scale=1.0 / n)
        recip(dst, dst, writes, writes)

    par = dsem["par"]

    def pload(dst, src):
        P.dma("sp", (lambda e: e.dma_start(out=dst, in_=src)), par)

    def rows(v):
        return v.rearrange("(c p) -> c p", p=128)

    pload(ident_f[:], ident_d)
    pload(R1[0:16, :], rows(ffn1_norm)); pload(R1[16:32, :], rows(mix_norm)); pload(R1[32:48, :], rows(ffn2_norm)); pload(R1[48:64, :], rows(final_norm))
    pload(PI[:, :], rows(pos_d))
    pload(R1[72:80, :], rows(out_norm_gmlp)); pload(R1[80:88, :], rows(out_norm_mla)); pload(R1[88:104, :], rows(c_d))
    pload(R2[:, :], rows(b_ada)[0:128, :]); pload(R3[:, :], rows(b_ada)[128:144, :])
    pload(S1[:, 0:1024].rearrange("p (h q) -> p h q", h=8), gmlp_w_s.rearrange("h p q -> p h q"))
    pload(bc1[:], gmlp_v_norm.partition_broadcast(128))
    pload(qlat_bc[:], q_lat_norm.partition_broadcast(128)); pload(kvlat_bc[:], kv_lat_norm.partition_broadcast(128))
    pload(krope_bc[:], k_rope_norm.partition_broadcast(128)); pload(qn_bc[:], q_nope_norm.partition_broadcast(128))
    pload(qr_bc[:], q_rope_norm.partition_broadcast(128)); pload(kn_bc[:], k_nope_norm.partition_broadcast(128))
    pload(invf_bc[:], invf_d.partition_broadcast(128))
    par_tok = ("par", par.n)
    parb.w = par_tok
    S1b.w = par_tok

    P.op("pool", (lambda e: e.memset(ones_f[:], 1.0)), writes=[onesb])
    P.op("pool", (lambda e: e.memset(ones_b[:], 1.0)), writes=[onesb])
    vcopy(ident_b[:], ident_f[:], [parb], [identb])
    vcopy(R1[64:72, :], PI[:, :], [parb], [r1b])
    tr(PSB[0][:, 0:104], R1[0:104, :], ident_f[0:104, 0:104], [parb, r1b, identb], [pb[0]], True)
    vcopy(cols[:], PSB[0][:, 0:104], [pb[0]], [colsb])
    tr(PSB[1][:, 0:128], R2[:, :], ident_f[:], [parb, identb], [pb[1]], False)
    tr(PSB[1][:, 128:144], R3[:, :], ident_f[0:16, 0:16], [parb, identb], [pb[1]], True)
    vcopy(badac[:], PSB[1][:, 0:144], [pb[1]], [colsb])
    for g in range(2):
        for j in range(4):
            hh = g * 4 + j
            tr(PSB[3 + g][:, j * 128:(j + 1) * 128], S1[:, hh * 128:(hh + 1) * 128], ident_f[:], [S1b, identb], [pb[3 + g]], j == 3)
        vcopy(wsT[:, g * 4:(g + 1) * 4, :], PSB[3 + g][:, :].rearrange("p (a b) -> p a b", a=4), [pb[3 + g]], [wsb])
    act(cact[:], cols[:, 88:104], AF.Silu, [colsb], [cactb])
    for t8 in range(8):
        ts(angT[:, t8, :], invf_bc[:], cols[:, 64 + t8:65 + t8], None, ALU.mult, ALU.bypass, [colsb, parb], [angb])
    TWO_PI = float(2 * np.pi)
    angf = angT[:].rearrange("p a b -> p (a b)")
    ITt = sb("ITt", [128, 256], I32)

    def sin_table(dst, src):
        ts(S2[:, 256:512], src, 1.0 / TWO_PI, None, ALU.mult, None, [S2b, angb], [S2b])
        vcopy(ITt[:, :], S2[:, 256:512], [S2b], [ittb])
        vcopy(S2[:, 256:512], ITt[:, :], [ittb], [S2b])
        stt(S2[:, 512:768], S2[:, 256:512], -TWO_PI, src, ALU.mult, ALU.add, [S2b, angb], [S2b])
        ts(S2[:, 512:768], S2[:, 512:768], -3.1415925, 3.1415925, ALU.max, ALU.min, [S2b], [S2b])
        act(dst, S2[:, 512:768], AF.Sin, [S2b], [trigb])

    ittb = Buf("itt")
    sin_table(sinT[:].rearrange("p a b -> p (a b)"), angf)
    ts(S2[:, 0:256], angf, float(np.pi / 2), None, ALU.add, None, [angb], [S2b])
    sin_table(cosT[:].rearrange("p a b -> p (a b)"), S2[:, 0:256])
    for b_ in (onesb, identb, wsb, trigb, parb, colsb):
        pass

    xsb = [Buf("xs0"), Buf("xs1")]
    bank_rot = [0, 1, 3, 4]
    ev = 0
    for t8 in range(8):
        s = t8 % 2
        xst = xsF[:, s * 2048:(s + 1) * 2048]
        P.dma("sp", (lambda e, xst=xst, t8=t8: e.dma_start(out=xst, in_=x_d[t8 * 128:(t8 + 1) * 128, :])), dsem[f"xs{s}"], writes=[xsb[s]])
        for cg in range(4):
            bk = bank_rot[(t8 * 4 + cg) % 4]
            for j in range(4):
                c = cg * 4 + j
                tr(PSB[bk][:, j * 128:(j + 1) * 128], xst[:, c * 128:(c + 1) * 128], ident_f[:], [xsb[s], identb], [pb[bk]], j == 3)
            outv = hF[:, :].rearrange("p (c t) -> p c t", c=NCH)[:, cg * 4:(cg + 1) * 4, t8 * 128:(t8 + 1) * 128]
            inv = PSB[bk][:, :].rearrange("p (a b) -> p a b", a=4)
            wr = [hb[cg * 4 + j][t8 // 4] for j in range(4)]
            if ev % 2 == 0:
                vcopy(outv, inv, [pb[bk]], wr)
            else:
                act(outv, inv, AF.Identity, [pb[bk]], wr)
            ev += 1

    for c_ in range(NCH):
        for h_ in range(2):
            inherit(ntb[c_][h_], xsb)

    ZW = 175
    S1h = S1.bitcast(BF16)
    Zt = S1h[:, 0:16 * ZW].rearrange("p (k w) -> p k w", k=NCH)
    MR = sb("MR", [128, 256])
    mrb = Buf("MR")
    dve((lambda e: e.memset(S1h[:, 0:16 * ZW], 0.0)), [S1b], [S1b])
    vcopy(Zt[:, :, 47], cact[:, :], [cactb, S1b], [S1b])

    def ada_group(j2):
        if j2 < 16:
            return j2, PSB[2][:, 0:256], j2 == 0
        if j2 < 24:
            return j2, PSB[2][:, 0:256], j2 == 16
        return j2 - 24, PSB[2][:, 256:512], j2 == 24

    def ada_consume(j2, it):
        rb, view = get_item(it)
        r, region, first = ada_group(j2)
        for k in range(NCH):
            mm(region, Zt[:, k, 47 - r:ZW - r], view[:, k, :], first and k == 0, (j2 in (15, 23, 71)) and k == NCH - 1, [rb, S1b], [pb[2]], inc=(k == NCH - 1), skip=True)
        release(it)

    def ada_evac(region, r0, r1, jb):
        vcopy(MR[:, :], region, [pb[2]], [mrb])
        tr(PSB[0][:, 0:128], MR[:, 0:128], ident_f[:], [mrb, parb], [pb[0]], False)
        tr(PSB[0][:, 128:256], MR[:, 128:256], ident_f[:], [mrb, parb], [pb[0]], True)
        nr = r1 - r0
        mv = modc[:, 2 * jb:2 * (jb + nr)].rearrange("p (r two) -> p r two", two=2)
        bv = badac[:, 2 * jb:2 * (jb + nr)].rearrange("p (r two) -> p r two", two=2)
        tt_(mv[:, :, 0], PSB[0][:, r0:r1], bv[:, :, 0], ALU.add, [pb[0], colsb], [modb])
        tt_(mv[:, :, 1], PSB[0][:, 128 + r0:128 + r1], bv[:, :, 1], ALU.add, [pb[0], colsb], [modb])

    def derive(idx, sc_lo, norm_lo):
        stt(der[:, idx, :], modc[:, sc_lo:sc_lo + 16], 1.0, cols[:, norm_lo:norm_lo + 16], ALU.add, ALU.mult, [modb, colsb], [derb])


    def norm_stats(src_fn, src_bufs, nchunks, nfeat):
        for half in range(2):
            for c in range(nchunks):
                s = c % 2
                act(sqr[:, s, :], src_fn(c, half), AF.Square, [src_bufs[c][half]], [sqb[s]])
                mm(PSB[2][:, :], ones_b[:], sqr[:, s, :], c == 0, c == nchunks - 1, [sqb[s], onesb], [pb[2]], inc=True)
            act(rstd_bc[:, half, :], PSB[2][:, :], AF.Sqrt, [pb[2]], [rsb[half]], bias=EPS, scale=1.0 / nfeat)
            recip(rstd_bc[:, half, :], rstd_bc[:, half, :], [rsb[half]], [rsb[half]])

    def norm_mod(a_cols, s_cols, do_stats=True):
        if do_stats:
            norm_stats(hv, hb, NCH, D)
        for half in range(2):
            for c in range(NCH):
                s = c % 2
                stt(tmpr[:, s, :], hv(c, half), a_cols[:, c:c + 1], rstd_bc[:, half, :], ALU.mult, ALU.mult,
                    [hb[c][half], rsb[half], derb], [tmpb[s]])
                act(ntv(c, half), tmpr[:, s, :], AF.Identity, [tmpb[s], modb], [ntb[c][half]], bias=s_cols[:, c:c + 1])

    GU = [(PSB[0], pb[0], PSB[1], pb[1]), (PSB[3], pb[3], PSB[4], pb[4])]
    DB = [(PA[:, 0:512], pa0), (PA[:, 512:1024], pa1)]
    actT = Bt[:, 0:4096].rearrange("p (f t) -> p f t", f=4)

    def ffn(k, gh_cols):
        for b in range(NBLK):
            blk = ffn_items[k][b]
            for fc in range(4):
                if DBG_SKIP_FFN:
                    break
                pr = fc // 2
                off = (fc % 2) * 128
                gb, gv = get_item(blk[("g", pr)])
                ub, uv = get_item(blk[("u", pr)])
                for half in range(2):
                    G, Gb, U, Ub = GU[(fc * 2 + half) % 2]
                    for kk in range(NCH):
                        mm(G[:, :], gv[:, kk, off:off + 128], ntv(kk, half), kk == 0, kk == NCH - 1, [gb, ntb[kk][half]], [Gb])
                    for kk in range(NCH):
                        mm(U[:, :], uv[:, kk, off:off + 128], ntv(kk, half), kk == 0, kk == NCH - 1, [ub, ntb[kk][half]], [Ub])
                    s = (fc * 2 + half) % 2
                    act(tmpr[:, s, :], G[:, :], AF.Silu, [Gb], [tmpb[s]])
                    tt_(actT[:, fc, half * 512:(half + 1) * 512], tmpr[:, s, :], U[:, :], ALU.mult, [tmpb[s], Ub], [actb[fc][half]])
                if fc % 2 == 1:
                    release(blk[("g", pr)])
                    release(blk[("u", pr)])
            if "ada_early" in blk:
                for (j2, it) in blk["ada_early"]:
                    ada_consume(j2, it)
                ada_evac(PSB[2][:, 0:256], 16, 24, 16)
                ts(der[:, 1, :], modc[:, 32:48], 0.5, None, ALU.mult, ALU.bypass, [modb], [derb])
            if not DBG_SKIP_FFN:
                d0b, d0v = get_item(blk[("d", 0)])
                d1b, d1v = get_item(blk[("d", 1)])
                dvs = [(d0b, d0v), (d1b, d1v)]
            for dc in range(NCH):
                if DBG_SKIP_FFN:
                    break
                for half in range(2):
                    Dv, Db = DB[(dc * 2 + half) % 2]
                    for fc in range(4):
                        db_, dv_ = dvs[fc // 2]
                        mm(Dv, dv_[:, fc % 2, dc * 128:(dc + 1) * 128], actT[:, fc, half * 512:(half + 1) * 512], fc == 0, fc == 3,
                           [db_, actb[fc][half]], [Db])
                    stt(hv(dc, half), Dv, gh_cols[:, dc:dc + 1], hv(dc, half), ALU.mult, ALU.add, [Db, hb[dc][half], derb, modb], [hb[dc][half]])
            if not DBG_SKIP_FFN:
                release(blk[("d", 0)])
                release(blk[("d", 1)])
            for (j2, it) in blk.get("ada", []):
                ada_consume(j2, it)

    norm_stats(hv, hb, NCH, D)
    for j2 in range(16):
        ada_consume(j2, ada_items[j2])
    ada_evac(PSB[2][:, 0:256], 0, 16, 0)
    derive(0, 16, 0)
    norm_mod(der[:, 0, :], modc[:, 0:16], do_stats=False)
    ffn(1, der[:, 1, :])
    ada_evac(PSB[2][:, 256:512], 0, 48, 24)
    derive(2, 64, 16)
    derive(3, 112, 32)
    ts(der[:, 4, :], modc[:, 128:144], 0.5, None, ALU.mult, ALU.bypass, [modb], [derb])

    allh = [hb[c][h] for c in range(NCH) for h in range(2)]

    if STAGE >= 2:
        norm_mod(der[:, 2, :], modc[:, 48:64])
        P.wait_all("sp", [b_.w for b_ in allh])
        for i4 in range(4):
            P.dma("sp", (lambda e, i4=i4: e.dma_start(out=hsp.ap()[:, i4 * 4096:(i4 + 1) * 4096], in_=hF[:, i4 * 4096:(i4 + 1) * 4096])), dsem["spill"])
        spill_tok = ("spill", dsem["spill"].n)
        for b_ in allh:
            b_.rs.append(spill_tok)
        uT = A[:, 0:8192].rearrange("p (c t) -> p c t", c=8)
        vN = A[:, 8192:16384].rearrange("p (t f) -> p t f", t=8)
        mT = A[:, 8192:16384].rearrange("p (c t) -> p c t", c=8)
        qTn = A[:, 16384:24576].rearrange("p (h t) -> p h t", h=8)
        qTp = A[:, 24576:28672].rearrange("p (h t) -> p h t", h=4)
        kpeA = A[:, 28672:32768]
        kvlA = Bt[:, 0:8192].rearrange("p (c t) -> p c t", c=2)
        ub = [Buf(f"uT{c}") for c in range(8)]
        vb = [Buf(f"vN{t}") for t in range(8)]
        mb = [[Buf(f"mT{h}_{q}") for q in range(2)] for h in range(8)]
        qTb = Buf("qT"); kpeAb = Buf("kpeA"); kvlAb = Buf("kvlA")
        qlb = [Buf(f"qlt{t}") for t in range(8)]
        ownb = Buf("own")
        for nb in ub + vb + [qTb, kpeAb]:
            inherit(nb, allh)
        inherit(kvlAb, [actb[f][h] for f in range(4) for h in range(2)])

        for _once in (0,):
            for i in range(4):
                rb, view = get_item(win_u[i])
                for jj in range(2):
                    uc = i * 2 + jj
                    for half in range(2):
                        bk = (uc * 2 + half) % 2
                        for kk in range(NCH):
                            mm(PSB[bk][:, :], view[:, kk, jj * 128:(jj + 1) * 128], ntv(kk, half), kk == 0, kk == NCH - 1, [rb, ntb[kk][half]], [pb[bk]])
                        act(uT[:, uc, half * 512:(half + 1) * 512], PSB[bk][:, :], AF.Gelu_apprx_tanh, [pb[bk]], [ub[uc]])
                release(win_u[i])

            if MIXCUT < 1:
                break
            vit = [get_item(i) for i in win_v]
            if DBGX == 2:
                for i in range(4):
                    rb, view = vit[i]
                    for jj in range(2):
                        for half in range(2):
                            for kk in range(NCH):
                                mm(PSB[3][:, :], view[:, kk, jj * 128:(jj + 1) * 128], ntv(kk, half), kk == 0, kk == NCH - 1, [rb, ntb[kk][half]], [pb[3]])
            for t8 in range(8 if DBGX == 0 else (1 if DBGX == 1 else 0)):
                for i in range(4):
                    rb, view = vit[i]
                    pbuf = pa0 if i < 2 else pa1
                    for kk in range(NCH):
                        if DBGH1:
                            mm(PSB[3 + i // 2][:, (i % 2) * 256:(i % 2 + 1) * 256], NT[:, kk * T + t8 * 128: kk * T + (t8 + 1) * 128], view[:, kk, :], kk == 0, kk == NCH - 1,
                               [rb, ntb[kk][t8 // 4]], [pb[3 + i // 2]])
                        else:
                            mm(PA[:, i * 256:(i + 1) * 256], NT[:, kk * T + t8 * 128: kk * T + (t8 + 1) * 128], view[:, kk, :], kk == 0, kk == NCH - 1,
                               [rb, ntb[kk][t8 // 4]], [pbuf])
                if DBGSUB >= 1:
                    act(S2[:, 0:512], PA[:, 0:512], AF.Gelu_apprx_tanh, [pa0], [S2b])
                    act(S2[:, 512:1024], PA[:, 512:1024], AF.Gelu_apprx_tanh, [pa1], [S2b])
                if DBGSUB >= 2:
                    tt_(S1[:, 0:1024], S2[:, :], S2[:, :], ALU.mult, [S2b], [S1b])
                    rsum(st[:, 0:1], S1[:, 0:1024], [S1b], [stb])
                if DBGSUB >= 3:
                    rstd_small(st[:, 1:2], st[:, 0:1], 1024.0, [stb], [stb])
                if DBGSUB >= 4:
                    stt(vN[:, t8, :], S2[:, :], st[:, 1:2], bc1[:, :], ALU.mult, ALU.mult, [S2b, stb, parb], [vb[t8]])
            for i in win_v:
                release(i)
            bc1b = Buf("bc1")
            inherit(bc1b, vb)
            if DBGSUB >= 5:
                P.dma("sp", (lambda e: e.dma_start(out=bc1[:], in_=gmlp_b_s.partition_broadcast(128))), dsem["reload"], writes=[bc1b])

            if MIXCUT < 2:
                break
            cq0 = get_item(win_cq[0]); cq1 = get_item(win_cq[1]); ckv = get_item(win_ckv); kpe = get_item(win_kpe)
            kvown = OWN[:, 0:2, :]
            kpeown = OWN[:, 2, :]
            def m1c_mm(t8):
                    tok = slice(t8 * 128, (t8 + 1) * 128)
                    for (rb, view), (dst, pbuf) in (((cq0), (PSB[3][:, 0:256], pb[3])), ((cq1), (PSB[3][:, 256:512], pb[3])),
                                                      ((ckv), (PSB[4][:, 0:256], pb[4])), ((kpe[0], kpe[1][:, :, 192:256]), (PSB[4][:, 256:320], pb[4]))):
                        for kk in range(NCH):
                            mm(dst, NT[:, kk * T + t8 * 128: kk * T + (t8 + 1) * 128], view[:, kk, :], kk == 0, kk == NCH - 1,
                               [rb, ntb[kk][t8 // 4]], [pbuf])
            def m1c_chain(t8):
                    tok = slice(t8 * 128, (t8 + 1) * 128)
                    act(S1[:, 0:512], PSB[3][:, :], AF.Identity, [pb[3]], [S1b])
                    act(S1[:, 512:832], PSB[4][:, 0:320], AF.Identity, [pb[4]], [S1b])
                    tt_(S2[:, 0:832], S1[:, 0:832], S1[:, 0:832], ALU.mult, [S1b], [S2b])
                    rsum(st[:, 2:3], S2[:, 0:512], [S2b], [stb])
                    rsum(st[:, 3:4], S2[:, 512:768], [S2b], [stb])
                    rsum(st[:, 4:5], S2[:, 768:832], [S2b], [stb])
                    rstd_small(st[:, 5:6], st[:, 2:3], 512.0, [stb], [stb])
                    rstd_small(st[:, 6:7], st[:, 3:4], 256.0, [stb], [stb])
                    rstd_small(st[:, 7:8], st[:, 4:5], 64.0, [stb], [stb])
                    qstage = qnb[:, 0:4, :]
                    stt(qstage.rearrange("p a b -> p (a b)"), S1[:, 0:512], st[:, 5:6], qlat_bc[:, :], ALU.mult, ALU.mult, [S1b, stb, parb], [qnbb])
                    kvstage = qnb[:, 4:6, :]
                    stt(kvstage.rearrange("p a b -> p (a b)"), S1[:, 512:768], st[:, 6:7], kvlat_bc[:, :], ALU.mult, ALU.mult, [S1b, stb, parb], [qnbb])
                    stt(S2[:, 0:64], S1[:, 768:832], st[:, 7:8], krope_bc[:, :], ALU.mult, ALU.mult, [S1b, stb, parb], [S2b])
                    x1 = S2[:, 0:32]; x2 = S2[:, 32:64]
                    tt_(S2[:, 64:96], x1, cosT[:, t8, :], ALU.mult, [S2b, trigb], [S2b])
                    tt_(S2[:, 96:128], x2, sinT[:, t8, :], ALU.mult, [S2b, trigb], [S2b])
                    tt_(S2[:, 128:160], x1, sinT[:, t8, :], ALU.mult, [S2b, trigb], [S2b])
                    tt_(S2[:, 160:192], x2, cosT[:, t8, :], ALU.mult, [S2b, trigb], [S2b])
                    tt_(kpe2[:, 0:32], S2[:, 64:96], S2[:, 96:128], ALU.subtract, [S2b], [kpe2b])
                    tt_(kpe2[:, 32:64], S2[:, 128:160], S2[:, 160:192], ALU.add, [S2b], [kpe2b])
                    vcopy(kpe2[:, 64:128], kpe2[:, 0:64], [kpe2b], [kpe2b])
            def m1c_tr(t8):
                    tok = slice(t8 * 128, (t8 + 1) * 128)
                    for j in range(4):
                        tr(TRB[:, j * 128:(j + 1) * 128], qnb[:, j, :], ident_b[:], [qnbb, identb], [trb], False)
                    for j in range(2):
                        tr(TRB[:, (4 + j) * 128:(5 + j) * 128], qnb[:, 4 + j, :], ident_b[:], [qnbb, identb], [trb], False)
                    tr(TRB[:, 6 * 128:7 * 128], kpe2[:, :], ident_b[:], [kpe2b, identb], [trb], True)
                    vcopy(QLT[:, :, tok], TRB[:, 0:512].rearrange("p (a b) -> p a b", a=4), [trb], [qlb[t8]])
                    vcopy(OWN[:, :, tok], TRB[:, 512:896].rearrange("p (a b) -> p a b", a=3), [trb], [ownb])

            for t8 in range(8):
                m1c_mm(t8)
                if t8 > 0:
                    m1c_tr(t8 - 1)
                m1c_chain(t8)
            m1c_tr(7)
            for i in win_cq + [win_ckv, win_kpe]:
                release(i)

            if MIXCUT < 3:
                break
            P.dma("sp", (lambda e: e.dma_start(out=gin.ap().rearrange("(c p) t -> p c t", p=128), in_=OWN[:, :, :])), dsem["gin"], reads=[ownb])
            gin_tok = ("gin", dsem["gin"].n)
            ccb = Buf("cc")
            P.dma("pool", (lambda e: e.collective_compute("AllGather", ALU.bypass, replica_groups=[[0, 1, 2, 3], [4, 5, 6, 7]],
                                                          ins=[gin.ap().opt()], outs=[gout.ap().opt()])), dsem["cc"], writes=[ccb], extra=[gin_tok], amt=1)
            if MIXCUT < 4:
                break
            for t8 in range(8):
                tok = slice(t8 * 128, (t8 + 1) * 128)
                for g in range(2):
                    bk = g
                    for j in range(4):
                        hg = g * 4 + j
                        mm(PSB[bk][:, j * 128:(j + 1) * 128], vN[:, t8, hg * 128:(hg + 1) * 128], wsT[:, hg, :], True, True, [vb[t8], wsb], [pb[bk]], inc=(j == 3))
                    s2v = S2[:, g * 512:(g + 1) * 512].rearrange("p (a b) -> p a b", a=4)
                    tt_(s2v, PSB[bk][:, :].rearrange("p (a b) -> p a b", a=4), bc1[:, g * 512:(g + 1) * 512].rearrange("p (a b) -> p a b", a=4), ALU.add,
                        [pb[bk], bc1b], [S2b])
                    uv_ = uT[:, g * 4:(g + 1) * 4, tok]
                    tt_(uv_, s2v, uv_, ALU.mult, [S2b] + ub[g * 4:(g + 1) * 4], ub[g * 4:(g + 1) * 4])

            if MIXCUT < 5:
                break
            q0b, q0v = get_item(wuq_items[0]); q1b, q1v = get_item(wuq_items[1])
            for t8 in range(8):
                tok = slice(t8 * 128, (t8 + 1) * 128)
                for n3 in range(3):
                    rb, view = (q0b, q0v) if n3 < 2 else (q1b, q1v)
                    if n3 == 0:
                        pieces = [(q0b, q0v[:, :, 0:512], PSB[0][:, :])]
                    elif n3 == 1:
                        pieces = [(q0b, q0v[:, :, 512:768], PSB[1][:, 0:256]), (q1b, q1v[:, :, 0:256], PSB[1][:, 256:512])]
                    else:
                        pieces = [(q1b, q1v[:, :, 256:768], PSB[3][:, :])]
                    bkb = [pb[0], pb[1], pb[3]][n3]
                    for (rb_, wv, dst) in pieces:
                        for kk in range(4):
                            mm(dst, QLT[:, kk, tok], wv[:, kk, :], kk == 0, kk == 3, [rb_, qlb[t8]], [bkb])
                act(S1[:, 0:512], PSB[0][:, :], AF.Identity, [pb[0]], [S1b])
                act(S1[:, 512:1024], PSB[1][:, :], AF.Identity, [pb[1]], [S1b])
                act(S1[:, 1024:1536], PSB[3][:, :], AF.Identity, [pb[3]], [S1b])
                q3 = S1[:, :].rearrange("p (h d) -> p h d", h=8)
                sq3 = S2[:, 0:1024].rearrange("p (h d) -> p h d", h=8)
                tt_(sq3, q3[:, :, 0:128], q3[:, :, 0:128], ALU.mult, [S1b], [S2b])
                rsum(st[:, 8:16], sq3, [S2b], [stb])
                rstd_small(st[:, 16:24], st[:, 8:16], 128.0, [stb], [stb])
                tt_(sq3, q3[:, :, 0:128], st[:, 16:24].unsqueeze(2).to_broadcast([128, 8, 128]), ALU.mult, [S1b, stb, S2b], [S2b])
                tt_(qnb[:, :, :], sq3, qn_bc[:, :].unsqueeze(1).to_broadcast([128, 8, 128]), ALU.mult, [S2b, parb], [qnbb])
                sp3 = S2[:, 0:512].rearrange("p (h d) -> p h d", h=8)
                tt_(sp3, q3[:, :, 128:192], q3[:, :, 128:192], ALU.mult, [S1b, S2b], [S2b])
                rsum(st[:, 24:32], sp3, [S2b], [stb])
                rstd_small(st[:, 32:40], st[:, 24:32], 64.0, [stb], [stb])
                tt_(sp3, q3[:, :, 128:192], st[:, 32:40].unsqueeze(2).to_broadcast([128, 8, 64]), ALU.mult, [S1b, stb, S2b], [S2b])
                tt_(sp3, sp3, qr_bc[:, :].unsqueeze(1).to_broadcast([128, 8, 64]), ALU.mult, [S2b, parb], [S2b])
                cb = cosT[:, t8, :].unsqueeze(1).to_broadcast([128, 8, 32])
                sbb = sinT[:, t8, :].unsqueeze(1).to_broadcast([128, 8, 32])
                r4 = tmpr[:, :, :].rearrange("p k (h d) -> p k h d", h=8)
                tt_(r4[:, 0, :, 0:32], sp3[:, :, 0:32], cb, ALU.mult, [S2b, trigb], tmpb)
                tt_(r4[:, 0, :, 32:64], sp3[:, :, 32:64], sbb, ALU.mult, [S2b, trigb], tmpb)
                tt_(r4[:, 1, :, 0:32], sp3[:, :, 0:32], sbb, ALU.mult, [S2b, trigb], tmpb)
                tt_(r4[:, 1, :, 32:64], sp3[:, :, 32:64], cb, ALU.mult, [S2b, trigb], tmpb)
                tt_(qpb[:, :, 0:32], r4[:, 0, :, 0:32], r4[:, 0, :, 32:64], ALU.subtract, tmpb, [qpbb])
                tt_(qpb[:, :, 32:64], r4[:, 1, :, 0:32], r4[:, 1, :, 32:64], ALU.add, tmpb, [qpbb])
                for hh in range(8):
                    tr(TRB[:, hh * 128:(hh + 1) * 128], qnb[:, hh, :], ident_b[:], [qnbb, identb], [trb], hh == 7)
                vcopy(qTn[:, :, tok], TRB[:, :].rearrange("p (a b) -> p a b", a=8), [trb], [qTb])
                qp2 = qpb[:, :, :].rearrange("p (a b) d -> p a (b d)", a=4)
                for pp in range(4):
                    tr(TRB[:, pp * 128:(pp + 1) * 128], qp2[:, pp, :], ident_b[:], [qpbb, identb], [trb], pp == 3)
                vcopy(qTp[:, :, tok], TRB[:, 0:512].rearrange("p (a b) -> p a b", a=4), [trb], [qTb])
            release(wuq_items[0]); release(wuq_items[1])

            if MIXCUT < 6:
                break
            inherit(kvlAb, [actb[f][h] for f in range(4) for h in range(2)] + qlb)
            P.wait_all("sp", [ccb.w] + kvlAb.rs + kpeAb.rs)
            for r in range(4):
                src_kv = gout.ap()[r * 384:r * 384 + 256, :].rearrange("(c p) t -> p c t", p=128)
                P.dma("sp", (lambda e, r=r, src_kv=src_kv: e.dma_start(out=kvlA[:, :, r * T:(r + 1) * T], in_=src_kv)), dsem["gout"])
                src_pe = gout.ap()[r * 384 + 256:r * 384 + 384, :]
                P.dma("sp", (lambda e, r=r, src_pe=src_pe: e.dma_start(out=kpeA[:, r * T:(r + 1) * T], in_=src_pe)), dsem["gout"])
            gout_tok = ("gout", dsem["gout"].n)
            kvlAb.w = gout_tok; kvlAb.rs = []
            kpeAb.w = gout_tok; kpeAb.rs = []

            kvb_, kvv = get_item(wukv_item)
            KT = [NT[:, s * 8192: s * 8192 + 4096] for s in range(2)]
            VV = [NT[:, s * 8192 + 4096: s * 8192 + 8192].rearrange("p (t d) -> p t d", t=32) for s in range(2)]
            ktb = [[Buf(f"KT{s}_{j}") for j in range(8)] for s in range(2)]
            vvb = [[Buf(f"VV{s}_{j}") for j in range(8)] for s in range(2)]
            allnt = [ntb[c][h] for c in range(NCH) for h in range(2)]
            for s in range(2):
                for j in range(8):
                    inherit(ktb[s][j], allnt)
                    inherit(vvb[s][j], allnt)
            prb = [Buf(f"pr{i}") for i in range(3)]
            for b_ in prb:
                inherit(b_, [ownb])
            for h8 in range(8):
                for q in range(2):
                    inherit(mb[h8][q], vb)
            SB_ = [(PSB[0], pb[0]), (PSB[1], pb[1]), (PSB[2], pb[2])]
            OB, LB = (PSB[3], pb[3]), (PSB[4], pb[4])

            def produce(h8, j):
                s = h8 % 2
                for i in range(4):
                    tile_ = j * 4 + i
                    for kk in range(2):
                        mm(PA[:, i * 256:(i + 1) * 256], kvlA[:, kk, tile_ * 128:(tile_ + 1) * 128], kvv[:, kk, h8 * 256:(h8 + 1) * 256], kk == 0, kk == 1,
                           [kvlAb, kvb_], [pa0 if i < 2 else pa1], inc=(kk == 1))
                ks = S1[:, 0:512].rearrange("p (a b) -> p a b", a=4)
                for hb_ in range(2):
                    kv2 = PA[:, hb_ * 512:(hb_ + 1) * 512].rearrange("p (a b) -> p a b", a=2)
                    pbk = pa0 if hb_ == 0 else pa1
                    act(ks[:, hb_ * 2:(hb_ + 1) * 2, :], kv2[:, :, 0:128], AF.Identity, [pbk], [S1b])
                    act(VV[s][:, j * 4 + hb_ * 2: j * 4 + hb_ * 2 + 2, :], kv2[:, :, 128:256], AF.Identity, [pbk], [vvb[s][j]])
                sq4 = S2[:, 0:512].rearrange("p (a b) -> p a b", a=4)
                tt_(sq4, ks, ks, ALU.mult, [S1b], [S2b])
                rsum(st[:, 40:44], sq4, [S2b], [stb])
                act(st[:, 44:48], st[:, 40:44], AF.Ln, [stb], [stb], bias=EPS, scale=1.0 / 128.0)
                act(st[:, 44:48], st[:, 44:48], AF.Exp, [stb], [stb], scale=-0.5)
                tt_(sq4, ks, st[:, 44:48].unsqueeze(2).to_broadcast([128, 4, 128]), ALU.mult, [S1b, stb, S2b], [S2b])
                tt_(knb[:, j % 2, :, :], sq4, kn_bc[:, :].unsqueeze(1).to_broadcast([128, 4, 128]), ALU.mult, [S2b, parb], [knbb[j % 2]])

            def produce_b(h8, j):
                s = h8 % 2
                for i in range(4):
                    tr(TRB[:, i * 128:(i + 1) * 128], knb[:, j % 2, i, :], ident_b[:], [knbb[j % 2], identb], [trb], i == 3)
                vcopy(KT[s][:, j * 512:(j + 1) * 512], TRB[:, 0:512], [trb], [ktb[s][j]])

            def s_mm(h8, q, kt):
                s = h8 % 2
                Sv, Sb = SB_[kt % 3]
                qs_ = slice(q * 512, (q + 1) * 512)
                mm(Sv[:, :], KT[s][:, kt * 128:(kt + 1) * 128], qTn[:, h8, qs_], True, False, [ktb[s][kt // 4], qTb], [Sb], inc=False)
                po = (h8 % 2) * 64
                mm(Sv[:, :], kpeA[po:po + 64, kt * 128:(kt + 1) * 128], qTp[po:po + 64, h8 // 2, qs_], False, True, [kpeAb, qTb], [Sb], inc=True)

            def pv_mm(h8, q, kt):
                s = h8 % 2
                Sv, Sb = SB_[kt % 3]
                pi = kt % 3
                act(PR[:, pi, :], Sv[:, :], AF.Exp, [Sb], [prb[pi]], scale=SM_SCALE)
                mm(OB[0][:, :], VV[s][:, kt, :], PR[:, pi, :], kt == 0, kt == 31, [vvb[s][kt // 4], prb[pi]], [OB[1]], inc=False)
                mm(LB[0][:, :], ones_b[:], PR[:, pi, :], kt == 0, kt == 31, [prb[pi], onesb], [LB[1]], inc=True)

            for j in range(8):
                produce(0, j)
                if j > 0:
                    produce_b(0, j - 1)
            produce_b(0, 7)
            for h8 in range(8):
                it = 0
                for q in range(2):
                    s_mm(h8, q, 0)
                    s_mm(h8, q, 1)
                    for kt in range(32):
                        if kt + 2 < 32:
                            s_mm(h8, q, kt + 2)
                        pv_mm(h8, q, kt)
                        it += 1
                        if h8 + 1 < 8 and it % 8 == 2:
                            produce(h8 + 1, it // 8)
                            if it // 8 > 0:
                                produce_b(h8 + 1, it // 8 - 1)
                    qs_ = slice(q * 512, (q + 1) * 512)
                    if q == 1 and h8 + 1 < 8:
                        produce_b(h8 + 1, 7)
                    recip(tmpr[:, 0, :], LB[0][:, :], [LB[1]], [tmpb[0]])
                    tt_(mT[:, h8, qs_], OB[0][:, :], tmpr[:, 0, :], ALU.mult, [OB[1], tmpb[0]], [mb[h8][q]])
            release(wukv_item)

            if MIXCUT < 7:
                break
            for c in range(NCH):
                for h in range(2):
                    inherit(ntb[c][h], [b_ for s in range(2) for b_ in ktb[s] + vvb[s]])
            ubh = [[ub[c], ub[c]] for c in range(8)]
            norm_stats(lambda c, half: uT[:, c, half * 512:(half + 1) * 512], ubh, 8, 1024.0)
            for half in range(2):
                for c in range(8):
                    stt(ntv(c, half), uT[:, c, half * 512:(half + 1) * 512], cols[:, 72 + c:73 + c], rstd_bc[:, half, :], ALU.mult, ALU.mult,
                        [ub[c], rsb[half], colsb], [ntb[c][half]])
            norm_stats(lambda c, half: mT[:, c, half * 512:(half + 1) * 512], mb, 8, 1024.0)
            for half in range(2):
                for c in range(8):
                    stt(ntv(8 + c, half), mT[:, c, half * 512:(half + 1) * 512], cols[:, 80 + c:81 + c], rstd_bc[:, half, :], ALU.mult, ALU.mult,
                        [mb[c][half], rsb[half], colsb], [ntb[8 + c][half]])
        mixbufs = ub + vb + [qTb, kpeAb] + [mb[h8][q] for h8 in range(8) for q in range(2)]
        for b_ in allh:
            inherit(b_, mixbufs)
        hb0 = Buf("hreload")
        inherit(hb0, mixbufs)
        P.wait_all("sp", [spill_tok] + hb0.rs)
        for i4 in range(4):
            P.dma("sp", (lambda e, i4=i4: e.dma_start(out=hF[:, i4 * 4096:(i4 + 1) * 4096], in_=hsp.ap()[:, i4 * 4096:(i4 + 1) * 4096])), dsem["spill"])
        rel_tok = ("spill", dsem["spill"].n)
        for b_ in allh:
            b_.w = rel_tok
            b_.rs = []
        for f in range(4):
            for h in range(2):
                inherit(actb[f][h], [kvlAb])
        if MIXCUT >= 8:
            for i in range(8):
                rb, view = get_item(wout_items[i])
                for jj in range(2):
                    dc = i * 2 + jj
                    for half in range(2):
                        bk = (dc * 2 + half) % 2
                        for kk in range(NCH):
                            mm(PSB[bk][:, :], view[:, kk, jj * 128:(jj + 1) * 128], ntv(kk, half), kk == 0, kk == NCH - 1, [rb, ntb[kk][half]], [pb[bk]])
                        stt(hv(dc, half), PSB[bk][:, :], modc[:, 80 + dc:81 + dc], hv(dc, half), ALU.mult, ALU.add, [pb[bk], hb[dc][half], modb], [hb[dc][half]])
                release(wout_items[i])

    if STAGE >= 3:
        norm_mod(der[:, 3, :], modc[:, 96:112])
        ffn(2, der[:, 4, :])

    norm_stats(hv, hb, NCH, D)
    for half in range(2):
        for c in range(NCH):
            stt(hv(c, half), hv(c, half), cols[:, 48 + c:49 + c], rstd_bc[:, half, :], ALU.mult, ALU.mult, [hb[c][half], rsb[half], colsb], [hb[c][half]])
    osb = [Buf("os0"), Buf("os1")]
    for b_ in osb:
        inherit(b_, [ntb[c][h] for c in range(NCH) for h in range(2)])
    ev = 0
    out_toks = []
    for t8 in range(8):
        s = t8 % 2
        ost = xsF[:, s * 2048:(s + 1) * 2048]
        for cg in range(4):
            bk = bank_rot[(t8 * 4 + cg) % 4]
            for j in range(4):
                c = cg * 4 + j
                tr(PSB[bk][:, j * 128:(j + 1) * 128], hF[:, c * T + t8 * 128: c * T + (t8 + 1) * 128], ident_f[:], [hb[c][t8 // 4], identb], [pb[bk]], j == 3)
            if ev % 2 == 0:
                vcopy(ost[:, cg * 512:(cg + 1) * 512], PSB[bk][:, :], [pb[bk]], [osb[s]])
            else:
                act(ost[:, cg * 512:(cg + 1) * 512], PSB[bk][:, :], AF.Identity, [pb[bk]], [osb[s]])
            ev += 1
        tk = P.dma("sp", (lambda e, ost=ost, t8=t8: e.dma_start(out=y_d[t8 * 128:(t8 + 1) * 128, :], in_=ost)), dsem[f"os{s}"], reads=[osb[s]])
        out_toks.append(tk)
    P.wait_all("sp", [("os0", dsem["os0"].n), ("os1", dsem["os1"].n)])

    engmap = {"pe": "tensor", "act": "scalar", "dve": "vector", "pool": "gpsimd", "sp": "sync"}
    with nc.Block() as block:
        def make(ename):
            def body(eng):
                for (ws, fn, inc) in P.ops[ename]:
                    for (sk, val) in ws:
                        eng.wait_ge(semh[sk], val)
                    if fn is None:
                        continue
                    ins = fn(eng)
                    if inc is not None:
                        ins.then_inc(semh[inc[0]], inc[1])
            return body
        for ename in Prog.ENG:
            getattr(block, engmap[ename])(make(ename))
    es.close()
    _CACHE['P'] = P
    return nc


_CACHE = {}


def kernel(**inputs):
    inp = {k: np.asarray(v) for k, v in inputs.items()}
    x = inp["x"]; c = inp["c"]; pos = inp["positions"]
    if "nc" not in _CACHE:
        _CACHE["nc"] = build_program()
    nc = _CACHE["nc"]
    shared = {}
    for k, v in inp.items():
        if k in ("x", "c", "positions"):
            continue
        a = np.ascontiguousarray(v[0])
        if k == "gmlp_b_s":
            a = a.reshape(-1)
        if DBG_SKIP_FFN and k == "w_ada":
            a = np.ascontiguousarray(a[:, :256])
        if DBG_SKIP_FFN and k.startswith("ffn") and "_w_" in k:
            a = np.ascontiguousarray(a[:128, :256])
        shared[k] = a
    shared["ident"] = np.eye(128, dtype=np.float32)
    shared["inv_freq"] = (10000.0 ** (-np.arange(0, 64, 2, dtype=np.float32) / 64)).astype(np.float32)
    in_maps = []
    for core in range(8):
        b, q = core // 4, core % 4
        m = dict(shared)
        m["x"] = np.ascontiguousarray(x[b, q * T:(q + 1) * T, :])
        m["c"] = np.ascontiguousarray(c[b])
        m["positions"] = np.ascontiguousarray(pos[b, q * T:(q + 1) * T]).astype(np.int32)
        in_maps.append(m)
    res = run_bass_kernel_spmd(nc, in_maps, core_ids=list(range(8)))
    out = np.empty((2, 4096, D), dtype=np.float32)
    for core in range(8):
        b, q = core // 4, core % 4
        out[b, q * T:(q + 1) * T, :] = np.asarray(res.results[core]["y"])
    return out
```

```python
import numpy as np
import ml_dtypes
from contextlib import ExitStack
import concourse.bass as bass
import concourse.mybir as mybir
from concourse.bass_utils import run_bass_kernel_spmd

F32 = mybir.dt.float32
BF16 = mybir.dt.bfloat16
I32 = mybir.dt.int32
AF = mybir.ActivationFunctionType
ALU = mybir.AluOpType
AX = mybir.AxisListType

D = 2048
T = 1024
NCH = 16
DFF = 5632
NBLK = 11
EPS = 1e-6
NS = 5
STAGE = 3
MIXCUT = 8
DBG_SKIP_FFN = False
DBGSUB = 9
DBGX = 0
DBGC = 9
DBGH1 = False
SAME_SYNC = True
SAME_RAW_ONLY = False
SM_SCALE = float(192 ** -0.5)


class Buf:
    __slots__ = ("name", "w", "rs", "frozen")

    def __init__(self, name):
        self.name = name
        self.w = None
        self.rs = []
        self.frozen = False


class DSem:
    def __init__(self, key):
        self.key = key
        self.n = 0


class Prog:
    ENG = ("pe", "act", "dve", "pool", "sp")

    def __init__(self):
        self.ops = {e: [] for e in self.ENG}
        self.cnt = {e: 0 for e in self.ENG}
        self.known = {e: {} for e in self.ENG}

    def _waits(self, eng, toks):
        out = []
        k = self.known[eng]
        for t in toks:
            if t is None:
                continue
            sem, val = t
            if sem == eng and (eng == "pe" or not SAME_SYNC):
                continue
            if k.get(sem, 0) >= val:
                continue
            k[sem] = val
            out.append((sem, val))
        return out

    def op(self, eng, fn, reads=(), writes=(), inc=True):
        toks = []
        for b in reads:
            toks.append(b.w)
        for b in writes:
            for t in [b.w] + b.rs:
                if SAME_RAW_ONLY and t is not None and t[0] == eng:
                    continue
                toks.append(t)
        ws = self._waits(eng, toks)
        if inc:
            self.cnt[eng] += 1
            tok = (eng, self.cnt[eng])
        else:
            tok = (eng, self.cnt[eng] + 1)
        self.ops[eng].append((ws, fn, (eng, 1) if inc else None))
        for b in reads:
            if not b.frozen:
                b.rs.append(tok)
        for b in writes:
            b.w = tok
            b.rs = []
        return tok

    def dma(self, eng, fn, dsem, reads=(), writes=(), extra=(), amt=16):
        toks = list(extra)
        for b in reads:
            toks.append(b.w)
        for b in writes:
            toks.append(b.w)
            toks.extend(b.rs)
        ws = self._waits(eng, toks)
        dsem.n += amt
        tok = (dsem.key, dsem.n)
        self.ops[eng].append((ws, fn, (dsem.key, amt)))
        for b in reads:
            if not b.frozen:
                b.rs.append(tok)
        for b in writes:
            b.w = tok
            b.rs = []
        return tok

    def wait_all(self, eng, toks):
        ws = self._waits(eng, toks)
        if ws:
            self.ops[eng].append((ws, None, None))


def inherit(new, olds):
    new.w = None
    rs = []
    for o in olds:
        if o.w is not None:
            rs.append(o.w)
        rs.extend(o.rs)
    new.rs = rs


def build_program():
    nc = bass.Bass("TRN2", target_bir_lowering=False)
    P = Prog()

    def din(name, shape, dt=F32):
        return nc.dram_tensor(name, list(shape), dt, kind="ExternalInput").ap()

    x_d = din("x", [T, D])
    c_d = din("c", [D])
    pos_d = din("positions", [T], I32)
    w_ada = din("w_ada", [D, 256 if DBG_SKIP_FFN else 9 * D])
    b_ada = din("b_ada", [9 * D])
    ffn_w = {}
    for k in (1, 2):
        if DBG_SKIP_FFN:
            ffn_w[k] = (din(f"ffn{k}_w_gate", [128, 256]), din(f"ffn{k}_w_up", [128, 256]), din(f"ffn{k}_w_down", [128, 256]))
        else:
            ffn_w[k] = (din(f"ffn{k}_w_gate", [D, DFF]), din(f"ffn{k}_w_up", [D, DFF]), din(f"ffn{k}_w_down", [DFF, D]))
    ffn1_norm = din("ffn1_norm", [D]); mix_norm = din("mix_norm", [D]); ffn2_norm = din("ffn2_norm", [D]); final_norm = din("final_norm", [D])
    w_in = din("w_in", [D, 2880])
    gmlp_v_norm = din("gmlp_v_norm", [1024]); gmlp_w_s = din("gmlp_w_s", [8, 128, 128]); gmlp_b_s = din("gmlp_b_s", [1024])
    q_lat_norm = din("q_lat_norm", [512]); w_uq = din("w_uq", [512, 1536])
    kv_lat_norm = din("kv_lat_norm", [256]); w_ukv = din("w_ukv", [256, 2048])
    q_nope_norm = din("q_nope_norm", [128]); q_rope_norm = din("q_rope_norm", [64])
    k_nope_norm = din("k_nope_norm", [128]); k_rope_norm = din("k_rope_norm", [64])
    out_norm_gmlp = din("out_norm_gmlp", [1024]); out_norm_mla = din("out_norm_mla", [1024])
    w_out = din("w_out", [D, D])
    ident_d = din("ident", [128, 128])
    invf_d = din("inv_freq", [32])
    y_d = nc.dram_tensor("y", [T, D], F32, kind="ExternalOutput").ap()
    hsp = nc.dram_tensor("hsp", [128, NCH * T], F32)
    gin = nc.dram_tensor("gin", [384, T], BF16)
    gout = nc.dram_tensor("gout", [4 * 384, T], BF16)

    es = ExitStack()

    def sb(name, shape, dt=F32):
        return es.enter_context(nc.sbuf_tensor(name, list(shape), dt))

    def ps(name, shape, dt=F32):
        return es.enter_context(nc.psum_tensor(name, list(shape), dt))

    sem_names = ["pe", "act", "dve", "pool"]
    dsem_keys = [f"ring{i}" for i in range(NS)] + ["par", "xs0", "xs1", "os0", "os1", "spill", "reload", "gin", "gout", "cc"]
    semh = {}
    for k in sem_names + dsem_keys:
        semh[k] = es.enter_context(nc.semaphore(k))
    dsem = {k: DSem(k) for k in dsem_keys}

    A = sb("A", [128, 32768], BF16)
    NT = sb("NT", [128, 16384], BF16)
    Bt = sb("Bt", [128, 8192], BF16)
    RING = sb("RING", [128, NS, 4096], BF16)
    OWN = sb("OWN", [128, 3, T], BF16)
    QLT = Bt[:, 4096:8192].rearrange("p (c t) -> p c t", c=4)
    PR = OWN[:, 0:2, :].rearrange("p a b -> p (a b)")[:, 0:1536].rearrange("p (a b) -> p a b", a=3)
    cols = sb("cols", [128, 104]); badac = sb("badac", [128, 144]); modc = sb("modc", [128, 144])
    der = sb("der", [128, 8, 16])
    cact = sb("cact", [128, 16], BF16)
    ident_f = sb("ident_f", [128, 128]); ident_b = sb("ident_b", [128, 128], BF16)
    ones_f = sb("ones_f", [128, 128]); ones_b = sb("ones_b", [128, 128], BF16)
    R1 = sb("R1", [128, 128]); R2 = sb("R2", [128, 128]); R3 = sb("R3", [16, 128]); PI = sb("PI", [8, 128], I32)
    wsT = sb("wsT", [128, 8, 128], BF16)
    bc1 = sb("bc1", [128, 1024])
    qlat_bc = sb("qlat_bc", [128, 512]); kvlat_bc = sb("kvlat_bc", [128, 256]); krope_bc = sb("krope_bc", [128, 64])
    qn_bc = sb("qn_bc", [128, 128]); qr_bc = sb("qr_bc", [128, 64]); kn_bc = sb("kn_bc", [128, 128]); invf_bc = sb("invf_bc", [128, 32])
    cosT = sb("cosT", [128, 8, 32]); sinT = sb("sinT", [128, 8, 32]); angT = sb("angT", [128, 8, 32])
    rstd_bc = sb("rstd_bc", [128, 2, 512])
    sqr = sb("sqr", [128, 2, 512], BF16); tmpr = sb("tmpr", [128, 2, 512])
    S1 = sb("S1", [128, 1536]); S2 = sb("S2", [128, 1024])
    st = sb("st", [128, 64])
    kpe2 = sb("kpe2", [128, 128], BF16)
    qnb = sb("qnb", [128, 8, 128], BF16); qpb = sb("qpb", [128, 8, 64], BF16)
    knb = sb("knb", [128, 2, 4, 128], BF16)

    PSB = [ps(f"P{i}", [128, 512]) for i in range(5)]
    PA = ps("PA", [128, 1024])
    TRB = ps("TRB", [128, 1024], BF16)
    pb = [Buf(f"P{i}") for i in range(5)]
    pa0, pa1 = Buf("PA0"), Buf("PA1")
    trb = Buf("TRB")

    hF = A.bitcast(F32)
    xsF = NT.bitcast(F32)

    def hv(c, half):
        return hF[:, c * T + half * 512: c * T + half * 512 + 512]

    def ntv(c, half):
        return NT[:, c * T + half * 512: c * T + half * 512 + 512]

    hb = [[Buf(f"h{c}_{h}") for h in range(2)] for c in range(NCH)]
    ntb = [[Buf(f"nt{c}_{h}") for h in range(2)] for c in range(NCH)]
    actb = [[Buf(f"act{f}_{h}") for h in range(2)] for f in range(4)]
    sqb = [Buf("sq0"), Buf("sq1")]
    tmpb = [Buf("tmp0"), Buf("tmp1")]
    rsb = [Buf("rs0"), Buf("rs1")]
    ring_b = [Buf(f"ring{i}") for i in range(NS)]
    parb = Buf("params")
    colsb = Buf("cols"); modb = Buf("modc"); derb = Buf("der"); cactb = Buf("cact"); onesb = Buf("ones"); identb = Buf("ident"); wsb = Buf("wsT"); trigb = Buf("trig"); angb = Buf("ang"); r1b = Buf("r1")
    qnbb, kpe2b, qpbb = Buf("qnb"), Buf("kpe2"), Buf("qpb")
    knbb = [Buf("knb0"), Buf("knb1")]
    S1b, S2b, stb = Buf("S1"), Buf("S2"), Buf("st")

    items = []
    state = {"issued": 0, "released": [False] * 4096}

    def add_item(src, a, b):
        items.append((src, a, b))
        return len(items) - 1

    slot_owner = [None] * NS
    slot_of = {}

    def pump():
        while state["issued"] < len(items):
            k = state["issued"]
            free = [s_ for s_ in range(NS) if slot_owner[s_] is None]
            if not free:
                break
            s = free[0]
            slot_owner[s] = k
            slot_of[k] = s
            src, a, b = items[k]
            dst = RING[:, s, 0:a * b].rearrange("p (a b) -> p a b", a=a)
            P.dma("pool", (lambda e, dst=dst, src=src: e.dma_start(out=dst, in_=src)), dsem[f"ring{s}"], writes=[ring_b[s]])
            state["issued"] += 1

    def get_item(k):
        pump()
        assert state["issued"] > k, f"ring item {k} not issued (ring too small)"
        src, a, b = items[k]
        s = slot_of[k]
        return ring_b[s], RING[:, s, 0:a * b].rearrange("p (a b) -> p a b", a=a)

    def release(k):
        state["released"][k] = True
        slot_owner[slot_of[k]] = None
        pump()

    def colitem(w, c0, n):
        return add_item(w[:, c0:c0 + n].rearrange("(k p) n -> p k n", p=128), w.shape[0] // 128, n)

    def rowitem(w, r0, nrow):
        return add_item(w[r0:r0 + nrow, :].rearrange("(j p) n -> p j n", p=128), nrow // 128, w.shape[1])

    ada_items = [colitem(w_ada, 0 if DBG_SKIP_FFN else j * 256, 256) for j in range(16)]
    ffn_items = {1: [], 2: []}
    ada_rest = []

    def add_ffn_items(k):
        Wg, Wu, Wd = ffn_w[k]
        for b in range(NBLK):
            blk = {}
            for pr in range(2):
                if DBG_SKIP_FFN:
                    break
                f0 = (b * 4 + pr * 2) * 128
                blk[("g", pr)] = colitem(Wg, f0, 256)
                blk[("u", pr)] = colitem(Wu, f0, 256)
            if k == 1 and b == 0:
                blk["ada_early"] = [(j, colitem(w_ada, 0 if DBG_SKIP_FFN else j * 256, 256)) for j in range(16, 24)]
            for pr in range(2):
                if DBG_SKIP_FFN:
                    break
                f0 = (b * 4 + pr * 2) * 128
                blk[("d", pr)] = rowitem(Wd, f0, 256)
            ffn_items[k].append(blk)
            if k == 1:
                lo = 24 + (24 * b) // NBLK
                hi = 24 + (24 * (b + 1)) // NBLK
                blk["ada"] = [(j, colitem(w_ada, 0 if DBG_SKIP_FFN else j * 256, 256)) for j in range(lo, hi)]

    add_ffn_items(1)
    win_u = [colitem(w_in, i * 256, 256) for i in range(4)]
    win_v = [colitem(w_in, 1024 + i * 256, 256) for i in range(4)]
    win_cq = [colitem(w_in, 2048 + i * 256, 256) for i in range(2)]
    win_ckv = colitem(w_in, 2560, 256)
    win_kpe = colitem(w_in, 2624, 256)
    n_items_dbg = len(items)
    ada_late_a = [(j, colitem(w_ada, 0 if DBG_SKIP_FFN else j * 256, 256)) for j in range(48, 60)]
    wuq_items = [colitem(w_uq, i * 768, 768) for i in range(2)]
    ada_late_b = [(j, colitem(w_ada, 0 if DBG_SKIP_FFN else j * 256, 256)) for j in range(60, 72)]
    wukv_item = rowitem(w_ukv, 0, 256)
    wout_items = [colitem(w_out, i * 256, 256) for i in range(8)]
    add_ffn_items(2)

    if DBGX == 5:
        del items[n_items_dbg:]

    def mm(out, lhsT, rhs, start, stop, reads, writes, inc=None):
        if inc is None:
            inc = stop
        P.op("pe", (lambda e: e.matmul(out, lhsT, rhs, start=start, stop=stop)), reads=reads, writes=writes, inc=inc)

    def tr(out, in_, ident, reads, writes, inc):
        P.op("pe", (lambda e: e.transpose(out, in_, ident)), reads=reads, writes=writes, inc=inc)

    def act(out, in_, func, reads, writes, bias=None, scale=None):
        kw = {}
        if bias is not None:
            kw["bias"] = bias
        if scale is not None:
            kw["scale"] = scale
        P.op("act", (lambda e: e.activation(out=out, in_=in_, func=func, **kw)), reads=reads, writes=writes)

    def dve(fn, reads, writes):
        P.op("dve", fn, reads=reads, writes=writes)

    def tt_(out, in0, in1, op, reads, writes):
        dve((lambda e: e.tensor_tensor(out=out, in0=in0, in1=in1, op=op)), reads, writes)

    def stt(out, in0, scalar, in1, op0, op1, reads, writes):
        dve((lambda e: e.scalar_tensor_tensor(out=out, in0=in0, scalar=scalar, in1=in1, op0=op0, op1=op1)), reads, writes)

    def ts(out, in0, s1, s2, op0, op1, reads, writes):
        if s2 is None:
            dve((lambda e: e.tensor_single_scalar(out=out, in_=in0, scalar=s1, op=op0)), reads, writes)
        else:
            dve((lambda e: e.tensor_scalar(out=out, in0=in0, scalar1=s1, scalar2=s2, op0=op0, op1=op1)), reads, writes)

    def vcopy(out, in_, reads, writes):
        dve((lambda e: e.tensor_copy(out=out, in_=in_)), reads, writes)

    def rsum(out, in_, reads, writes):
        dve((lambda e: e.tensor_reduce(out=out, in_=in_, axis=AX.X, op=ALU.add)), reads, writes)

    def recip(out, in_, reads, writes):
        dve((lambda e: e.reciprocal(out=out, in_=in_)), reads, writes)

    def rstd_small(dst, ss, n, reads, writes):
        act(dst, ss, AF.Sqrt, reads, writes, bias=EPS, scale=1.0 / n)
        recip(dst, dst, writes, writes)

    par = dsem["par"]

    def pload(dst, src):
        P.dma("sp", (lambda e: e.dma_start(out=dst, in_=src)), par)

    def rows(v):
        return v.rearrange("(c p) -> c p", p=128)

    pload(ident_f[:], ident_d)
    pload(R1[0:16, :], rows(ffn1_norm)); pload(R1[16:32, :], rows(mix_norm)); pload(R1[32:48, :], rows(ffn2_norm)); pload(R1[48:64, :], rows(final_norm))
    pload(PI[:, :], rows(pos_d))
    pload(R1[72:80, :], rows(out_norm_gmlp)); pload(R1[80:88, :], rows(out_norm_mla)); pload(R1[88:104, :], rows(c_d))
    pload(R2[:, :], rows(b_ada)[0:128, :]); pload(R3[:, :], rows(b_ada)[128:144, :])
    pload(S1[:, 0:1024].rearrange("p (h q) -> p h q", h=8), gmlp_w_s.rearrange("h p q -> p h q"))
    pload(bc1[:], gmlp_v_norm.partition_broadcast(128))
    pload(qlat_bc[:], q_lat_norm.partition_broadcast(128)); pload(kvlat_bc[:], kv_lat_norm.partition_broadcast(128))
    pload(krope_bc[:], k_rope_norm.partition_broadcast(128)); pload(qn_bc[:], q_nope_norm.partition_broadcast(128))
    pload(qr_bc[:], q_rope_norm.partition_broadcast(128)); pload(kn_bc[:], k_nope_norm.partition_broadcast(128))
    pload(invf_bc[:], invf_d.partition_broadcast(128))
    par_tok = ("par", par.n)
    parb.w = par_tok
    S1b.w = par_tok

    P.op("pool", (lambda e: e.memset(ones_f[:], 1.0)), writes=[onesb])
    P.op("pool", (lambda e: e.memset(ones_b[:], 1.0)), writes=[onesb])
    vcopy(ident_b[:], ident_f[:], [parb], [identb])
    vcopy(R1[64:72, :], PI[:, :], [parb], [r1b])
    tr(PSB[0][:, 0:104], R1[0:104, :], ident_f[0:104, 0:104], [parb, r1b, identb], [pb[0]], True)
    vcopy(cols[:], PSB[0][:, 0:104], [pb[0]], [colsb])
    tr(PSB[1][:, 0:128], R2[:, :], ident_f[:], [parb, identb], [pb[1]], False)
    tr(PSB[1][:, 128:144], R3[:, :], ident_f[0:16, 0:16], [parb, identb], [pb[1]], True)
    vcopy(badac[:], PSB[1][:, 0:144], [pb[1]], [colsb])
    for g in range(2):
        for j in range(4):
            hh = g * 4 + j
            tr(PSB[3 + g][:, j * 128:(j + 1) * 128], S1[:, hh * 128:(hh + 1) * 128], ident_f[:], [S1b, identb], [pb[3 + g]], j == 3)
        vcopy(wsT[:, g * 4:(g + 1) * 4, :], PSB[3 + g][:, :].rearrange("p (a b) -> p a b", a=4), [pb[3 + g]], [wsb])
    act(cact[:], cols[:, 88:104], AF.Silu, [colsb], [cactb])
    for t8 in range(8):
        ts(angT[:, t8, :], invf_bc[:], cols[:, 64 + t8:65 + t8], None, ALU.mult, ALU.bypass, [colsb, parb], [angb])
    TWO_PI = float(2 * np.pi)
    angf = angT[:].rearrange("p a b -> p (a b)")
    ITt = sb("ITt", [128, 256], I32)

    def sin_table(dst, src):
        ts(S2[:, 256:512], src, 1.0 / TWO_PI, None, ALU.mult, None, [S2b, angb], [S2b])
        vcopy(ITt[:, :], S2[:, 256:512], [S2b], [ittb])
        vcopy(S2[:, 256:512], ITt[:, :], [ittb], [S2b])
        stt(S2[:, 512:768], S2[:, 256:512], -TWO_PI, src, ALU.mult, ALU.add, [S2b, angb], [S2b])
        ts(S2[:, 512:768], S2[:, 512:768], -3.1415925, 3.1415925, ALU.max, ALU.min, [S2b], [S2b])
        act(dst, S2[:, 512:768], AF.Sin, [S2b], [trigb])

    ittb = Buf("itt")
    sin_table(sinT[:].rearrange("p a b -> p (a b)"), angf)
    ts(S2[:, 0:256], angf, float(np.pi / 2), None, ALU.add, None, [angb], [S2b])
    sin_table(cosT[:].rearrange("p a b -> p (a b)"), S2[:, 0:256])
    for b_ in (onesb, identb, wsb, trigb, parb, colsb):
        pass

    xsb = [Buf("xs0"), Buf("xs1")]
    bank_rot = [0, 1, 3, 4]
    ev = 0
    for t8 in range(8):
        s = t8 % 2
        xst = xsF[:, s * 2048:(s + 1) * 2048]
        P.dma("sp", (lambda e, xst=xst, t8=t8: e.dma_start(out=xst, in_=x_d[t8 * 128:(t8 + 1) * 128, :])), dsem[f"xs{s}"], writes=[xsb[s]])
        for cg in range(4):
            bk = bank_rot[(t8 * 4 + cg) % 4]
            for j in range(4):
                c = cg * 4 + j
                tr(PSB[bk][:, j * 128:(j + 1) * 128], xst[:, c * 128:(c + 1) * 128], ident_f[:], [xsb[s], identb], [pb[bk]], j == 3)
            outv = hF[:, :].rearrange("p (c t) -> p c t", c=NCH)[:, cg * 4:(cg + 1) * 4, t8 * 128:(t8 + 1) * 128]
            inv = PSB[bk][:, :].rearrange("p (a b) -> p a b", a=4)
            wr = [hb[cg * 4 + j][t8 // 4] for j in range(4)]
            if ev % 2 == 0:
                vcopy(outv, inv, [pb[bk]], wr)
            else:
                act(outv, inv, AF.Identity, [pb[bk]], wr)
            ev += 1

    for c_ in range(NCH):
        for h_ in range(2):
            inherit(ntb[c_][h_], xsb)

    def ada_consume(j2, it):
        rb, view = get_item(it)
        for jj in range(2):
            j = j2 * 2 + jj
            for k in range(NCH):
                mm(PSB[2][:, j:j + 1], view[:, k, jj * 128:(jj + 1) * 128], cact[:, k:k + 1], k == 0, k == NCH - 1,
                   [rb, cactb], [pb[2]], inc=(k == NCH - 1))
        release(it)


    def derive(idx, sc_lo, norm_lo):
        stt(der[:, idx, :], modc[:, sc_lo:sc_lo + 16], 1.0, cols[:, norm_lo:norm_lo + 16], ALU.add, ALU.mult, [modb, colsb], [derb])


    def norm_stats(src_fn, src_bufs, nchunks, nfeat):
        for half in range(2):
            for c in range(nchunks):
                s = c % 2
                act(sqr[:, s, :], src_fn(c, half), AF.Square, [src_bufs[c][half]], [sqb[s]])
                mm(PSB[2][:, :], ones_b[:], sqr[:, s, :], c == 0, c == nchunks - 1, [sqb[s], onesb], [pb[2]], inc=True)
            act(rstd_bc[:, half, :], PSB[2][:, :], AF.Sqrt, [pb[2]], [rsb[half]], bias=EPS, scale=1.0 / nfeat)
            recip(rstd_bc[:, half, :], rstd_bc[:, half, :], [rsb[half]], [rsb[half]])

    def norm_mod(a_cols, s_cols, do_stats=True):
        if do_stats:
            norm_stats(hv, hb, NCH, D)
        for half in range(2):
            for c in range(NCH):
                s = c % 2
                stt(tmpr[:, s, :], hv(c, half), a_cols[:, c:c + 1], rstd_bc[:, half, :], ALU.mult, ALU.mult,
                    [hb[c][half], rsb[half], derb], [tmpb[s]])
                act(ntv(c, half), tmpr[:, s, :], AF.Identity, [tmpb[s], modb], [ntb[c][half]], bias=s_cols[:, c:c + 1])

    GU = [(PSB[0], pb[0], PSB[1], pb[1]), (PSB[3], pb[3], PSB[4], pb[4])]
    DB = [(PA[:, 0:512], pa0), (PA[:, 512:1024], pa1)]
    actT = Bt[:, 0:4096].rearrange("p (f t) -> p f t", f=4)

    def ffn(k, gh_cols):
        for b in range(NBLK):
            blk = ffn_items[k][b]
            for fc in range(4):
                if DBG_SKIP_FFN:
                    break
                pr = fc // 2
                off = (fc % 2) * 128
                gb, gv = get_item(blk[("g", pr)])
                ub, uv = get_item(blk[("u", pr)])
                for half in range(2):
                    G, Gb, U, Ub = GU[(fc * 2 + half) % 2]
                    for kk in range(NCH):
                        mm(G[:, :], gv[:, kk, off:off + 128], ntv(kk, half), kk == 0, kk == NCH - 1, [gb, ntb[kk][half]], [Gb])
                    for kk in range(NCH):
                        mm(U[:, :], uv[:, kk, off:off + 128], ntv(kk, half), kk == 0, kk == NCH - 1, [ub, ntb[kk][half]], [Ub])
                    s = (fc * 2 + half) % 2
                    act(tmpr[:, s, :], G[:, :], AF.Silu, [Gb], [tmpb[s]])
                    tt_(actT[:, fc, half * 512:(half + 1) * 512], tmpr[:, s, :], U[:, :], ALU.mult, [tmpb[s], Ub], [actb[fc][half]])
                if fc % 2 == 1:
                    release(blk[("g", pr)])
                    release(blk[("u", pr)])
            if "ada_early" in blk:
                for (j2, it) in blk["ada_early"]:
                    ada_consume(j2, it)
                tt_(modc[:, 32:48], PSB[2][:, 32:48], badac[:, 32:48], ALU.add, [pb[2], colsb], [modb])
                ts(der[:, 1, :], modc[:, 32:48], 0.5, None, ALU.mult, ALU.bypass, [modb], [derb])
            if not DBG_SKIP_FFN:
                d0b, d0v = get_item(blk[("d", 0)])
                d1b, d1v = get_item(blk[("d", 1)])
                dvs = [(d0b, d0v), (d1b, d1v)]
            for dc in range(NCH):
                if DBG_SKIP_FFN:
                    break
                for half in range(2):
                    Dv, Db = DB[(dc * 2 + half) % 2]
                    for fc in range(4):
                        db_, dv_ = dvs[fc // 2]
                        mm(Dv, dv_[:, fc % 2, dc * 128:(dc + 1) * 128], actT[:, fc, half * 512:(half + 1) * 512], fc == 0, fc == 3,
                           [db_, actb[fc][half]], [Db])
                    stt(hv(dc, half), Dv, gh_cols[:, dc:dc + 1], hv(dc, half), ALU.mult, ALU.add, [Db, hb[dc][half], derb, modb], [hb[dc][half]])
            if not DBG_SKIP_FFN:
                release(blk[("d", 0)])
                release(blk[("d", 1)])
            for (j2, it) in blk.get("ada", []):
                ada_consume(j2, it)

    norm_stats(hv, hb, NCH, D)
    for j2 in range(16):
        ada_consume(j2, ada_items[j2])
    tt_(modc[:, 0:32], PSB[2][:, 0:32], badac[:, 0:32], ALU.add, [pb[2], colsb], [modb])
    derive(0, 16, 0)
    norm_mod(der[:, 0, :], modc[:, 0:16], do_stats=False)
    ffn(1, der[:, 1, :])
    tt_(modc[:, 48:96], PSB[2][:, 48:96], badac[:, 48:96], ALU.add, [pb[2], colsb], [modb])
    derive(2, 64, 16)

    allh = [hb[c][h] for c in range(NCH) for h in range(2)]

    if STAGE >= 2:
        norm_mod(der[:, 2, :], modc[:, 48:64])
        P.wait_all("sp", [b_.w for b_ in allh])
        for i4 in range(4):
            P.dma("sp", (lambda e, i4=i4: e.dma_start(out=hsp.ap()[:, i4 * 4096:(i4 + 1) * 4096], in_=hF[:, i4 * 4096:(i4 + 1) * 4096])), dsem["spill"])
        spill_tok = ("spill", dsem["spill"].n)
        for b_ in allh:
            b_.rs.append(spill_tok)
        uT = A[:, 0:8192].rearrange("p (c t) -> p c t", c=8)
        vN = A[:, 8192:16384].rearrange("p (t f) -> p t f", t=8)
        mT = A[:, 8192:16384].rearrange("p (c t) -> p c t", c=8)
        qTn = A[:, 16384:24576].rearrange("p (h t) -> p h t", h=8)
        qTp = A[:, 24576:28672].rearrange("p (h t) -> p h t", h=4)
        kpeA = A[:, 28672:32768]
        kvlA = Bt[:, 0:8192].rearrange("p (c t) -> p c t", c=2)
        ub = [Buf(f"uT{c}") for c in range(8)]
        vb = [Buf(f"vN{t}") for t in range(8)]
        mb = [[Buf(f"mT{h}_{q}") for q in range(2)] for h in range(8)]
        qTb = Buf("qT"); kpeAb = Buf("kpeA"); kvlAb = Buf("kvlA")
        qlb = [Buf(f"qlt{t}") for t in range(8)]
        ownb = Buf("own")
        for nb in ub + vb + [qTb, kpeAb]:
            inherit(nb, allh)
        inherit(kvlAb, [actb[f][h] for f in range(4) for h in range(2)])

        for _once in (0,):
            for i in range(4):
                rb, view = get_item(win_u[i])
                for jj in range(2):
                    uc = i * 2 + jj
                    for half in range(2):
                        bk = (uc * 2 + half) % 2
                        for kk in range(NCH):
                            mm(PSB[bk][:, :], view[:, kk, jj * 128:(jj + 1) * 128], ntv(kk, half), kk == 0, kk == NCH - 1, [rb, ntb[kk][half]], [pb[bk]])
                        act(uT[:, uc, half * 512:(half + 1) * 512], PSB[bk][:, :], AF.Gelu_apprx_tanh, [pb[bk]], [ub[uc]])
                release(win_u[i])

            if MIXCUT < 1:
                break
            vit = [get_item(i) for i in win_v]
            if DBGX == 2:
                for i in range(4):
                    rb, view = vit[i]
                    for jj in range(2):
                        for half in range(2):
                            for kk in range(NCH):
                                mm(PSB[3][:, :], view[:, kk, jj * 128:(jj + 1) * 128], ntv(kk, half), kk == 0, kk == NCH - 1, [rb, ntb[kk][half]], [pb[3]])
            for t8 in range(8 if DBGX == 0 else (1 if DBGX == 1 else 0)):
                for i in range(4):
                    rb, view = vit[i]
                    pbuf = pa0 if i < 2 else pa1
                    for kk in range(NCH):
                        if DBGH1:
                            mm(PSB[3 + i // 2][:, (i % 2) * 256:(i % 2 + 1) * 256], NT[:, kk * T + t8 * 128: kk * T + (t8 + 1) * 128], view[:, kk, :], kk == 0, kk == NCH - 1,
                               [rb, ntb[kk][t8 // 4]], [pb[3 + i // 2]])
                        else:
                            mm(PA[:, i * 256:(i + 1) * 256], NT[:, kk * T + t8 * 128: kk * T + (t8 + 1) * 128], view[:, kk, :], kk == 0, kk == NCH - 1,
                               [rb, ntb[kk][t8 // 4]], [pbuf])
                if DBGSUB >= 1:
                    act(S2[:, 0:512], PA[:, 0:512], AF.Gelu_apprx_tanh, [pa0], [S2b])
                    act(S2[:, 512:1024], PA[:, 512:1024], AF.Gelu_apprx_tanh, [pa1], [S2b])
                if DBGSUB >= 2:
                    tt_(S1[:, 0:1024], S2[:, :], S2[:, :], ALU.mult, [S2b], [S1b])
                    rsum(st[:, 0:1], S1[:, 0:1024], [S1b], [stb])
                if DBGSUB >= 3:
                    rstd_small(st[:, 1:2], st[:, 0:1], 1024.0, [stb], [stb])
                if DBGSUB >= 4:
                    stt(vN[:, t8, :], S2[:, :], st[:, 1:2], bc1[:, :], ALU.mult, ALU.mult, [S2b, stb, parb], [vb[t8]])
            for i in win_v:
                release(i)
            bc1b = Buf("bc1")
            inherit(bc1b, vb)
            if DBGSUB >= 5:
                P.dma("sp", (lambda e: e.dma_start(out=bc1[:], in_=gmlp_b_s.partition_broadcast(128))), dsem["reload"], writes=[bc1b])

            if MIXCUT < 2:
                break
            cq0 = get_item(win_cq[0]); cq1 = get_item(win_cq[1]); ckv = get_item(win_ckv); kpe = get_item(win_kpe)
            kvown = OWN[:, 0:2, :]
            kpeown = OWN[:, 2, :]
            def m1c_mm(t8):
                    tok = slice(t8 * 128, (t8 + 1) * 128)
                    for (rb, view), (dst, pbuf) in (((cq0), (PSB[3][:, 0:256], pb[3])), ((cq1), (PSB[3][:, 256:512], pb[3])),
                                                      ((ckv), (PSB[4][:, 0:256], pb[4])), ((kpe[0], kpe[1][:, :, 192:256]), (PSB[4][:, 256:320], pb[4]))):
                        for kk in range(NCH):
                            mm(dst, NT[:, kk * T + t8 * 128: kk * T + (t8 + 1) * 128], view[:, kk, :], kk == 0, kk == NCH - 1,
                               [rb, ntb[kk][t8 // 4]], [pbuf])
            def m1c_chain(t8):
                    tok = slice(t8 * 128, (t8 + 1) * 128)
                    act(S1[:, 0:512], PSB[3][:, :], AF.Identity, [pb[3]], [S1b])
                    act(S1[:, 512:832], PSB[4][:, 0:320], AF.Identity, [pb[4]], [S1b])
                    tt_(S2[:, 0:832], S1[:, 0:832], S1[:, 0:832], ALU.mult, [S1b], [S2b])
                    rsum(st[:, 2:3], S2[:, 0:512], [S2b], [stb])
                    rsum(st[:, 3:4], S2[:, 512:768], [S2b], [stb])
                    rsum(st[:, 4:5], S2[:, 768:832], [S2b], [stb])
                    rstd_small(st[:, 5:6], st[:, 2:3], 512.0, [stb], [stb])
                    rstd_small(st[:, 6:7], st[:, 3:4], 256.0, [stb], [stb])
                    rstd_small(st[:, 7:8], st[:, 4:5], 64.0, [stb], [stb])
                    qstage = qnb[:, 0:4, :]
                    stt(qstage.rearrange("p a b -> p (a b)"), S1[:, 0:512], st[:, 5:6], qlat_bc[:, :], ALU.mult, ALU.mult, [S1b, stb, parb], [qnbb])
                    kvstage = qnb[:, 4:6, :]
                    stt(kvstage.rearrange("p a b -> p (a b)"), S1[:, 512:768], st[:, 6:7], kvlat_bc[:, :], ALU.mult, ALU.mult, [S1b, stb, parb], [qnbb])
                    stt(S2[:, 0:64], S1[:, 768:832], st[:, 7:8], krope_bc[:, :], ALU.mult, ALU.mult, [S1b, stb, parb], [S2b])
                    x1 = S2[:, 0:32]; x2 = S2[:, 32:64]
                    tt_(S2[:, 64:96], x1, cosT[:, t8, :], ALU.mult, [S2b, trigb], [S2b])
                    tt_(S2[:, 96:128], x2, sinT[:, t8, :], ALU.mult, [S2b, trigb], [S2b])
                    tt_(S2[:, 128:160], x1, sinT[:, t8, :], ALU.mult, [S2b, trigb], [S2b])
                    tt_(S2[:, 160:192], x2, cosT[:, t8, :], ALU.mult, [S2b, trigb], [S2b])
                    tt_(kpe2[:, 0:32], S2[:, 64:96], S2[:, 96:128], ALU.subtract, [S2b], [kpe2b])
                    tt_(kpe2[:, 32:64], S2[:, 128:160], S2[:, 160:192], ALU.add, [S2b], [kpe2b])
                    vcopy(kpe2[:, 64:128], kpe2[:, 0:64], [kpe2b], [kpe2b])
            def m1c_tr(t8):
                    tok = slice(t8 * 128, (t8 + 1) * 128)
                    for j in range(4):
                        tr(TRB[:, j * 128:(j + 1) * 128], qnb[:, j, :], ident_b[:], [qnbb, identb], [trb], False)
                    for j in range(2):
                        tr(TRB[:, (4 + j) * 128:(5 + j) * 128], qnb[:, 4 + j, :], ident_b[:], [qnbb, identb], [trb], False)
                    tr(TRB[:, 6 * 128:7 * 128], kpe2[:, :], ident_b[:], [kpe2b, identb], [trb], True)
                    vcopy(QLT[:, :, tok], TRB[:, 0:512].rearrange("p (a b) -> p a b", a=4), [trb], [qlb[t8]])
                    vcopy(OWN[:, :, tok], TRB[:, 512:896].rearrange("p (a b) -> p a b", a=3), [trb], [ownb])

            for t8 in range(8):
                m1c_mm(t8)
                if t8 > 0:
                    m1c_tr(t8 - 1)
                m1c_chain(t8)
            m1c_tr(7)
            for i in win_cq + [win_ckv, win_kpe]:
                release(i)

            if MIXCUT < 3:
                break
            P.dma("sp", (lambda e: e.dma_start(out=gin.ap().rearrange("(c p) t -> p c t", p=128), in_=OWN[:, :, :])), dsem["gin"], reads=[ownb])
            gin_tok = ("gin", dsem["gin"].n)
            ccb = Buf("cc")
            P.dma("pool", (lambda e: e.collective_compute("AllGather", ALU.bypass, replica_groups=[[0, 1, 2, 3], [4, 5, 6, 7]],
                                                          ins=[gin.ap().opt()], outs=[gout.ap().opt()])), dsem["cc"], writes=[ccb], extra=[gin_tok], amt=1)
            if MIXCUT < 4:
                break
            for t8 in range(8):
                for (j2_, it_) in ada_late_a[(12 * t8) // 8:(12 * (t8 + 1)) // 8]:
                    ada_consume(j2_, it_)
                tok = slice(t8 * 128, (t8 + 1) * 128)
                for g in range(2):
                    bk = g
                    for j in range(4):
                        hg = g * 4 + j
                        mm(PSB[bk][:, j * 128:(j + 1) * 128], vN[:, t8, hg * 128:(hg + 1) * 128], wsT[:, hg, :], True, True, [vb[t8], wsb], [pb[bk]], inc=(j == 3))
                    s2v = S2[:, g * 512:(g + 1) * 512].rearrange("p (a b) -> p a b", a=4)
                    tt_(s2v, PSB[bk][:, :].rearrange("p (a b) -> p a b", a=4), bc1[:, g * 512:(g + 1) * 512].rearrange("p (a b) -> p a b", a=4), ALU.add,
                        [pb[bk], bc1b], [S2b])
                    uv_ = uT[:, g * 4:(g + 1) * 4, tok]
                    tt_(uv_, s2v, uv_, ALU.mult, [S2b] + ub[g * 4:(g + 1) * 4], ub[g * 4:(g + 1) * 4])

            if MIXCUT < 5:
                break
            q0b, q0v = get_item(wuq_items[0]); q1b, q1v = get_item(wuq_items[1])
            for t8 in range(8):
                for (j2_, it_) in ada_late_b[(12 * t8) // 8:(12 * (t8 + 1)) // 8]:
                    ada_consume(j2_, it_)
                tok = slice(t8 * 128, (t8 + 1) * 128)
                for n3 in range(3):
                    rb, view = (q0b, q0v) if n3 < 2 else (q1b, q1v)
                    if n3 == 0:
                        pieces = [(q0b, q0v[:, :, 0:512], PSB[0][:, :])]
                    elif n3 == 1:
                        pieces = [(q0b, q0v[:, :, 512:768], PSB[1][:, 0:256]), (q1b, q1v[:, :, 0:256], PSB[1][:, 256:512])]
                    else:
                        pieces = [(q1b, q1v[:, :, 256:768], PSB[3][:, :])]
                    bkb = [pb[0], pb[1], pb[3]][n3]
                    for (rb_, wv, dst) in pieces:
                        for kk in range(4):
                            mm(dst, QLT[:, kk, tok], wv[:, kk, :], kk == 0, kk == 3, [rb_, qlb[t8]], [bkb])
                act(S1[:, 0:512], PSB[0][:, :], AF.Identity, [pb[0]], [S1b])
                act(S1[:, 512:1024], PSB[1][:, :], AF.Identity, [pb[1]], [S1b])
                act(S1[:, 1024:1536], PSB[3][:, :], AF.Identity, [pb[3]], [S1b])
                q3 = S1[:, :].rearrange("p (h d) -> p h d", h=8)
                sq3 = S2[:, 0:1024].rearrange("p (h d) -> p h d", h=8)
                tt_(sq3, q3[:, :, 0:128], q3[:, :, 0:128], ALU.mult, [S1b], [S2b])
                rsum(st[:, 8:16], sq3, [S2b], [stb])
                rstd_small(st[:, 16:24], st[:, 8:16], 128.0, [stb], [stb])
                tt_(sq3, q3[:, :, 0:128], st[:, 16:24].unsqueeze(2).to_broadcast([128, 8, 128]), ALU.mult, [S1b, stb, S2b], [S2b])
                tt_(qnb[:, :, :], sq3, qn_bc[:, :].unsqueeze(1).to_broadcast([128, 8, 128]), ALU.mult, [S2b, parb], [qnbb])
                sp3 = S2[:, 0:512].rearrange("p (h d) -> p h d", h=8)
                tt_(sp3, q3[:, :, 128:192], q3[:, :, 128:192], ALU.mult, [S1b, S2b], [S2b])
                rsum(st[:, 24:32], sp3, [S2b], [stb])
                rstd_small(st[:, 32:40], st[:, 24:32], 64.0, [stb], [stb])
                tt_(sp3, q3[:, :, 128:192], st[:, 32:40].unsqueeze(2).to_broadcast([128, 8, 64]), ALU.mult, [S1b, stb, S2b], [S2b])
                tt_(sp3, sp3, qr_bc[:, :].unsqueeze(1).to_broadcast([128, 8, 64]), ALU.mult, [S2b, parb], [S2b])
                cb = cosT[:, t8, :].unsqueeze(1).to_broadcast([128, 8, 32])
                sbb = sinT[:, t8, :].unsqueeze(1).to_broadcast([128, 8, 32])
                r4 = tmpr[:, :, :].rearrange("p k (h d) -> p k h d", h=8)
                tt_(r4[:, 0, :, 0:32], sp3[:, :, 0:32], cb, ALU.mult, [S2b, trigb], tmpb)
                tt_(r4[:, 0, :, 32:64], sp3[:, :, 32:64], sbb, ALU.mult, [S2b, trigb], tmpb)
                tt_(r4[:, 1, :, 0:32], sp3[:, :, 0:32], sbb, ALU.mult, [S2b, trigb], tmpb)
                tt_(r4[:, 1, :, 32:64], sp3[:, :, 32:64], cb, ALU.mult, [S2b, trigb], tmpb)
                tt_(qpb[:, :, 0:32], r4[:, 0, :, 0:32], r4[:, 0, :, 32:64], ALU.subtract, tmpb, [qpbb])
                tt_(qpb[:, :, 32:64], r4[:, 1, :, 0:32], r4[:, 1, :, 32:64], ALU.add, tmpb, [qpbb])
                for hh in range(8):
                    tr(TRB[:, hh * 128:(hh + 1) * 128], qnb[:, hh, :], ident_b[:], [qnbb, identb], [trb], hh == 7)
                vcopy(qTn[:, :, tok], TRB[:, :].rearrange("p (a b) -> p a b", a=8), [trb], [qTb])
                qp2 = qpb[:, :, :].rearrange("p (a b) d -> p a (b d)", a=4)
                for pp in range(4):
                    tr(TRB[:, pp * 128:(pp + 1) * 128], qp2[:, pp, :], ident_b[:], [qpbb, identb], [trb], pp == 3)
                vcopy(qTp[:, :, tok], TRB[:, 0:512].rearrange("p (a b) -> p a b", a=4), [trb], [qTb])
            release(wuq_items[0]); release(wuq_items[1])
            tt_(modc[:, 96:144], PSB[2][:, 96:144], badac[:, 96:144], ALU.add, [pb[2], colsb], [modb])
            derive(3, 112, 32)
            ts(der[:, 4, :], modc[:, 128:144], 0.5, None, ALU.mult, ALU.bypass, [modb], [derb])

            if MIXCUT < 6:
                break
            inherit(kvlAb, [actb[f][h] for f in range(4) for h in range(2)] + qlb)
            P.wait_all("sp", [ccb.w] + kvlAb.rs + kpeAb.rs)
            for r in range(4):
                src_kv = gout.ap()[r * 384:r * 384 + 256, :].rearrange("(c p) t -> p c t", p=128)
                P.dma("sp", (lambda e, r=r, src_kv=src_kv: e.dma_start(out=kvlA[:, :, r * T:(r + 1) * T], in_=src_kv)), dsem["gout"])
                src_pe = gout.ap()[r * 384 + 256:r * 384 + 384, :]
                P.dma("sp", (lambda e, r=r, src_pe=src_pe: e.dma_start(out=kpeA[:, r * T:(r + 1) * T], in_=src_pe)), dsem["gout"])
            gout_tok = ("gout", dsem["gout"].n)
            kvlAb.w = gout_tok; kvlAb.rs = []
            kpeAb.w = gout_tok; kpeAb.rs = []

            kvb_, kvv = get_item(wukv_item)
            KT = [NT[:, s * 8192: s * 8192 + 4096] for s in range(2)]
            VV = [NT[:, s * 8192 + 4096: s * 8192 + 8192].rearrange("p (t d) -> p t d", t=32) for s in range(2)]
            ktb = [[Buf(f"KT{s}_{j}") for j in range(8)] for s in range(2)]
            vvb = [[Buf(f"VV{s}_{j}") for j in range(8)] for s in range(2)]
            allnt = [ntb[c][h] for c in range(NCH) for h in range(2)]
            for s in range(2):
                for j in range(8):
                    inherit(ktb[s][j], allnt)
                    inherit(vvb[s][j], allnt)
            prb = [Buf(f"pr{i}") for i in range(3)]
            for b_ in prb:
                inherit(b_, [ownb])
            for h8 in range(8):
                for q in range(2):
                    inherit(mb[h8][q], vb)
            SB_ = [(PSB[0], pb[0]), (PSB[1], pb[1]), (PSB[2], pb[2])]
            OB, LB = (PSB[3], pb[3]), (PSB[4], pb[4])

            def produce(h8, j):
                s = h8 % 2
                for i in range(4):
                    tile_ = j * 4 + i
                    for kk in range(2):
                        mm(PA[:, i * 256:(i + 1) * 256], kvlA[:, kk, tile_ * 128:(tile_ + 1) * 128], kvv[:, kk, h8 * 256:(h8 + 1) * 256], kk == 0, kk == 1,
                           [kvlAb, kvb_], [pa0 if i < 2 else pa1], inc=(kk == 1))
                ks = S1[:, 0:512].rearrange("p (a b) -> p a b", a=4)
                for hb_ in range(2):
                    kv2 = PA[:, hb_ * 512:(hb_ + 1) * 512].rearrange("p (a b) -> p a b", a=2)
                    pbk = pa0 if hb_ == 0 else pa1
                    act(ks[:, hb_ * 2:(hb_ + 1) * 2, :], kv2[:, :, 0:128], AF.Identity, [pbk], [S1b])
                    act(VV[s][:, j * 4 + hb_ * 2: j * 4 + hb_ * 2 + 2, :], kv2[:, :, 128:256], AF.Identity, [pbk], [vvb[s][j]])
                sq4 = S2[:, 0:512].rearrange("p (a b) -> p a b", a=4)
                tt_(sq4, ks, ks, ALU.mult, [S1b], [S2b])
                rsum(st[:, 40:44], sq4, [S2b], [stb])
                act(st[:, 44:48], st[:, 40:44], AF.Ln, [stb], [stb], bias=EPS, scale=1.0 / 128.0)
                act(st[:, 44:48], st[:, 44:48], AF.Exp, [stb], [stb], scale=-0.5)
                tt_(sq4, ks, st[:, 44:48].unsqueeze(2).to_broadcast([128, 4, 128]), ALU.mult, [S1b, stb, S2b], [S2b])
                tt_(knb[:, j % 2, :, :], sq4, kn_bc[:, :].unsqueeze(1).to_broadcast([128, 4, 128]), ALU.mult, [S2b, parb], [knbb[j % 2]])

            def produce_b(h8, j):
                s = h8 % 2
                for i in range(4):
                    tr(TRB[:, i * 128:(i + 1) * 128], knb[:, j % 2, i, :], ident_b[:], [knbb[j % 2], identb], [trb], i == 3)
                vcopy(KT[s][:, j * 512:(j + 1) * 512], TRB[:, 0:512], [trb], [ktb[s][j]])

            def s_mm(h8, q, kt):
                s = h8 % 2
                Sv, Sb = SB_[kt % 3]
                qs_ = slice(q * 512, (q + 1) * 512)
                mm(Sv[:, :], KT[s][:, kt * 128:(kt + 1) * 128], qTn[:, h8, qs_], True, False, [ktb[s][kt // 4], qTb], [Sb], inc=False)
                po = (h8 % 2) * 64
                mm(Sv[:, :], kpeA[po:po + 64, kt * 128:(kt + 1) * 128], qTp[po:po + 64, h8 // 2, qs_], False, True, [kpeAb, qTb], [Sb], inc=True)

            def pv_mm(h8, q, kt):
                s = h8 % 2
                Sv, Sb = SB_[kt % 3]
                pi = kt % 3
                act(PR[:, pi, :], Sv[:, :], AF.Exp, [Sb], [prb[pi]], scale=SM_SCALE)
                mm(OB[0][:, :], VV[s][:, kt, :], PR[:, pi, :], kt == 0, kt == 31, [vvb[s][kt // 4], prb[pi]], [OB[1]], inc=False)
                mm(LB[0][:, :], ones_b[:], PR[:, pi, :], kt == 0, kt == 31, [prb[pi], onesb], [LB[1]], inc=True)

            for j in range(8):
                produce(0, j)
                if j > 0:
                    produce_b(0, j - 1)
            produce_b(0, 7)
            for h8 in range(8):
                it = 0
                for q in range(2):
                    s_mm(h8, q, 0)
                    s_mm(h8, q, 1)
                    for kt in range(32):
                        if kt + 2 < 32:
                            s_mm(h8, q, kt + 2)
                        pv_mm(h8, q, kt)
                        it += 1
                        if h8 + 1 < 8 and it % 8 == 2:
                            produce(h8 + 1, it // 8)
                            if it // 8 > 0:
                                produce_b(h8 + 1, it // 8 - 1)
                    qs_ = slice(q * 512, (q + 1) * 512)
                    if q == 1 and h8 + 1 < 8:
                        produce_b(h8 + 1, 7)
                    recip(tmpr[:, 0, :], LB[0][:, :], [LB[1]], [tmpb[0]])
                    tt_(mT[:, h8, qs_], OB[0][:, :], tmpr[:, 0, :], ALU.mult, [OB[1], tmpb[0]], [mb[h8][q]])
            release(wukv_item)

            if MIXCUT < 7:
                break
            for c in range(NCH):
                for h in range(2):
                    inherit(ntb[c][h], [b_ for s in range(2) for b_ in ktb[s] + vvb[s]])
            ubh = [[ub[c], ub[c]] for c in range(8)]
            norm_stats(lambda c, half: uT[:, c, half * 512:(half + 1) * 512], ubh, 8, 1024.0)
            for half in range(2):
                for c in range(8):
                    stt(ntv(c, half), uT[:, c, half * 512:(half + 1) * 512], cols[:, 72 + c:73 + c], rstd_bc[:, half, :], ALU.mult, ALU.mult,
                        [ub[c], rsb[half], colsb], [ntb[c][half]])
            norm_stats(lambda c, half: mT[:, c, half * 512:(half + 1) * 512], mb, 8, 1024.0)
            for half in range(2):
                for c in range(8):
                    stt(ntv(8 + c, half), mT[:, c, half * 512:(half + 1) * 512], cols[:, 80 + c:81 + c], rstd_bc[:, half, :], ALU.mult, ALU.mult,
                        [mb[c][half], rsb[half], colsb], [ntb[8 + c][half]])
        mixbufs = ub + vb + [qTb, kpeAb] + [mb[h8][q] for h8 in range(8) for q in range(2)]
        for b_ in allh:
            inherit(b_, mixbufs)
        hb0 = Buf("hreload")
        inherit(hb0, mixbufs)
        P.wait_all("sp", [spill_tok] + hb0.rs)
        for i4 in range(4):
            P.dma("sp", (lambda e, i4=i4: e.dma_start(out=hF[:, i4 * 4096:(i4 + 1) * 4096], in_=hsp.ap()[:, i4 * 4096:(i4 + 1) * 4096])), dsem["spill"])
        rel_tok = ("spill", dsem["spill"].n)
        for b_ in allh:
            b_.w = rel_tok
            b_.rs = []
        for f in range(4):
            for h in range(2):
                inherit(actb[f][h], [kvlAb])
        if MIXCUT >= 8:
            for i in range(8):
                rb, view = get_item(wout_items[i])
                for jj in range(2):
                    dc = i * 2 + jj
                    for half in range(2):
                        bk = (dc * 2 + half) % 2
                        for kk in range(NCH):
                            mm(PSB[bk][:, :], view[:, kk, jj * 128:(jj + 1) * 128], ntv(kk, half), kk == 0, kk == NCH - 1, [rb, ntb[kk][half]], [pb[bk]])
                        stt(hv(dc, half), PSB[bk][:, :], modc[:, 80 + dc:81 + dc], hv(dc, half), ALU.mult, ALU.add, [pb[bk], hb[dc][half], modb], [hb[dc][half]])
                release(wout_items[i])

    if STAGE >= 3:
        norm_mod(der[:, 3, :], modc[:, 96:112])
        ffn(2, der[:, 4, :])

    norm_stats(hv, hb, NCH, D)
    for half in range(2):
        for c in range(NCH):
            stt(hv(c, half), hv(c, half), cols[:, 48 + c:49 + c], rstd_bc[:, half, :], ALU.mult, ALU.mult, [hb[c][half], rsb[half], colsb], [hb[c][half]])
    osb = [Buf("os0"), Buf("os1")]
    for b_ in osb:
        inherit(b_, [ntb[c][h] for c in range(NCH) for h in range(2)])
    ev = 0
    out_toks = []
    for t8 in range(8):
        s = t8 % 2
        ost = xsF[:, s * 2048:(s + 1) * 2048]
        for cg in range(4):
            bk = bank_rot[(t8 * 4 + cg) % 4]
            for j in range(4):
                c = cg * 4 + j
                tr(PSB[bk][:, j * 128:(j + 1) * 128], hF[:, c * T + t8 * 128: c * T + (t8 + 1) * 128], ident_f[:], [hb[c][t8 // 4], identb], [pb[bk]], j == 3)
            if ev % 2 == 0:
                vcopy(ost[:, cg * 512:(cg + 1) * 512], PSB[bk][:, :], [pb[bk]], [osb[s]])
            else:
                act(ost[:, cg * 512:(cg + 1) * 512], PSB[bk][:, :], AF.Identity, [pb[bk]], [osb[s]])
            ev += 1
        tk = P.dma("sp", (lambda e, ost=ost, t8=t8: e.dma_start(out=y_d[t8 * 128:(t8 + 1) * 128, :], in_=ost)), dsem[f"os{s}"], reads=[osb[s]])
        out_toks.append(tk)
    P.wait_all("sp", [("os0", dsem["os0"].n), ("os1", dsem["os1"].n)])

    engmap = {"pe": "tensor", "act": "scalar", "dve": "vector", "pool": "gpsimd", "sp": "sync"}
    with nc.Block() as block:
        def make(ename):
            def body(eng):
                for (ws, fn, inc) in P.ops[ename]:
                    for (sk, val) in ws:
                        eng.wait_ge(semh[sk], val)
                    if fn is None:
                        continue
                    ins = fn(eng)
                    if inc is not None:
                        ins.then_inc(semh[inc[0]], inc[1])
            return body
        for ename in Prog.ENG:
            getattr(block, engmap[ename])(make(ename))
    es.close()
    _CACHE['P'] = P
    return nc


_CACHE = {}


def kernel(**inputs):
    inp = {k: np.asarray(v) for k, v in inputs.items()}
    x = inp["x"]; c = inp["c"]; pos = inp["positions"]
    if "nc" not in _CACHE:
        _CACHE["nc"] = build_program()
    nc = _CACHE["nc"]
    shared = {}
    for k, v in inp.items():
        if k in ("x", "c", "positions"):
            continue
        a = np.ascontiguousarray(v[0])
        if k == "gmlp_b_s":
            a = a.reshape(-1)
        if DBG_SKIP_FFN and k == "w_ada":
            a = np.ascontiguousarray(a[:, :256])
        if DBG_SKIP_FFN and k.startswith("ffn") and "_w_" in k:
            a = np.ascontiguousarray(a[:128, :256])
        shared[k] = a
    shared["ident"] = np.eye(128, dtype=np.float32)
    shared["inv_freq"] = (10000.0 ** (-np.arange(0, 64, 2, dtype=np.float32) / 64)).astype(np.float32)
    in_maps = []
    for core in range(8):
        b, q = core // 4, core % 4
        m = dict(shared)
        m["x"] = np.ascontiguousarray(x[b, q * T:(q + 1) * T, :])
        m["c"] = np.ascontiguousarray(c[b])
        m["positions"] = np.ascontiguousarray(pos[b, q * T:(q + 1) * T]).astype(np.int32)
        in_maps.append(m)
    res = run_bass_kernel_spmd(nc, in_maps, core_ids=list(range(8)))
    out = np.empty((2, 4096, D), dtype=np.float32)
    for core in range(8):
        b, q = core // 4, core % 4
        out[b, q * T:(q + 1) * T, :] = np.asarray(res.results[core]["y"])
    return out
```

```python
import numpy as np
import ml_dtypes
from contextlib import ExitStack
import concourse.bass as bass
import concourse.mybir as mybir
from concourse.bass_utils import run_bass_kernel_spmd

F32 = mybir.dt.float32
BF16 = mybir.dt.bfloat16
I32 = mybir.dt.int32
AF = mybir.ActivationFunctionType
ALU = mybir.AluOpType
AX = mybir.AxisListType

D = 2048
T = 1024
NCH = 16
DFF = 5632
NBLK = 11
EPS = 1e-6
NS = 5
STAGE = 3
MIXCUT = 8
DBG_SKIP_FFN = False
DBGSUB = 9
DBGX = 0
DBGC = 9
DBGH1 = False
SAME_SYNC = True
SAME_RAW_ONLY = False
SM_SCALE = float(192 ** -0.5)


class Buf:
    __slots__ = ("name", "w", "rs", "frozen")

    def __init__(self, name):
        self.name = name
        self.w = None
        self.rs = []
        self.frozen = False


class DSem:
    def __init__(self, key):
        self.key = key
        self.n = 0


class Prog:
    ENG = ("pe", "act", "dve", "pool", "sp")

    def __init__(self):
        self.ops = {e: [] for e in self.ENG}
        self.cnt = {e: 0 for e in self.ENG}
        self.known = {e: {} for e in self.ENG}

    def _waits(self, eng, toks):
        out = []
        k = self.known[eng]
        for t in toks:
            if t is None:
                continue
            sem, val = t
            if sem == eng and (eng == "pe" or not SAME_SYNC):
                continue
            if k.get(sem, 0) >= val:
                continue
            k[sem] = val
            out.append((sem, val))
        return out

    def op(self, eng, fn, reads=(), writes=(), inc=True):
        toks = []
        for b in reads:
            toks.append(b.w)
        for b in writes:
            for t in [b.w] + b.rs:
                if SAME_RAW_ONLY and t is not None and t[0] == eng:
                    continue
                toks.append(t)
        ws = self._waits(eng, toks)
        if inc:
            self.cnt[eng] += 1
            tok = (eng, self.cnt[eng])
        else:
            tok = (eng, self.cnt[eng] + 1)
        self.ops[eng].append((ws, fn, (eng, 1) if inc else None))
        for b in reads:
            if not b.frozen:
                b.rs.append(tok)
        for b in writes:
            b.w = tok
            b.rs = []
        return tok

    def dma(self, eng, fn, dsem, reads=(), writes=(), extra=(), amt=16):
        toks = list(extra)
        for b in reads:
            toks.append(b.w)
        for b in writes:
            toks.append(b.w)
            toks.extend(b.rs)
        ws = self._waits(eng, toks)
        dsem.n += amt
        tok = (dsem.key, dsem.n)
        self.ops[eng].append((ws, fn, (dsem.key, amt)))
        for b in reads:
            if not b.frozen:
                b.rs.append(tok)
        for b in writes:
            b.w = tok
            b.rs = []
        return tok

    def wait_all(self, eng, toks):
        ws = self._waits(eng, toks)
        if ws:
            self.ops[eng].append((ws, None, None))


def inherit(new, olds):
    new.w = None
    rs = []
    for o in olds:
        if o.w is not None:
            rs.append(o.w)
        rs.extend(o.rs)
    new.rs = rs


def build_program():
    nc = bass.Bass("TRN2", target_bir_lowering=False)
    P = Prog()

    def din(name, shape, dt=F32):
        return nc.dram_tensor(name, list(shape), dt, kind="ExternalInput").ap()

    x_d = din("x", [T, D])
    c_d = din("c", [D])
    pos_d = din("positions", [T], I32)
    w_ada = din("w_ada", [D, 256 if DBG_SKIP_FFN else 9 * D])
    b_ada = din("b_ada", [9 * D])
    ffn_w = {}
    for k in (1, 2):
        if DBG_SKIP_FFN:
            ffn_w[k] = (din(f"ffn{k}_w_gate", [128, 256]), din(f"ffn{k}_w_up", [128, 256]), din(f"ffn{k}_w_down", [128, 256]))
        else:
            ffn_w[k] = (din(f"ffn{k}_w_gate", [D, DFF]), din(f"ffn{k}_w_up", [D, DFF]), din(f"ffn{k}_w_down", [DFF, D]))
    ffn1_norm = din("ffn1_norm", [D]); mix_norm = din("mix_norm", [D]); ffn2_norm = din("ffn2_norm", [D]); final_norm = din("final_norm", [D])
    w_in = din("w_in", [D, 2880])
    gmlp_v_norm = din("gmlp_v_norm", [1024]); gmlp_w_s = din("gmlp_w_s", [8, 128, 128]); gmlp_b_s = din("gmlp_b_s", [1024])
    q_lat_norm = din("q_lat_norm", [512]); w_uq = din("w_uq", [512, 1536])
    kv_lat_norm = din("kv_lat_norm", [256]); w_ukv = din("w_ukv", [256, 2048])
    q_nope_norm = din("q_nope_norm", [128]); q_rope_norm = din("q_rope_norm", [64])
    k_nope_norm = din("k_nope_norm", [128]); k_rope_norm = din("k_rope_norm", [64])
    out_norm_gmlp = din("out_norm_gmlp", [1024]); out_norm_mla = din("out_norm_mla", [1024])
    w_out = din("w_out", [D, D])
    ident_d = din("ident", [128, 128])
    invf_d = din("inv_freq", [32])
    y_d = nc.dram_tensor("y", [T, D], F32, kind="ExternalOutput").ap()
    hsp = nc.dram_tensor("hsp", [128, NCH * T], F32)
    gin = nc.dram_tensor("gin", [384, T], BF16)
    gout = nc.dram_tensor("gout", [4 * 384, T], BF16)

    es = ExitStack()

    def sb(name, shape, dt=F32):
        return es.enter_context(nc.sbuf_tensor(name, list(shape), dt))

    def ps(name, shape, dt=F32):
        return es.enter_context(nc.psum_tensor(name, list(shape), dt))

    sem_names = ["pe", "act", "dve", "pool"]
    dsem_keys = [f"ring{i}" for i in range(NS)] + ["par", "xs0", "xs1", "os0", "os1", "spill", "reload", "gin", "gout", "cc"]
    semh = {}
    for k in sem_names + dsem_keys:
        semh[k] = es.enter_context(nc.semaphore(k))
    dsem = {k: DSem(k) for k in dsem_keys}

    A = sb("A", [128, 32768], BF16)
    NT = sb("NT", [128, 16384], BF16)
    Bt = sb("Bt", [128, 8192], BF16)
    RING = sb("RING", [128, NS, 4096], BF16)
    OWN = sb("OWN", [128, 3, T], BF16)
    QLT = Bt[:, 4096:8192].rearrange("p (c t) -> p c t", c=4)
    PR = OWN[:, 0:2, :].rearrange("p a b -> p (a b)")[:, 0:1536].rearrange("p (a b) -> p a b", a=3)
    cols = sb("cols", [128, 104]); badac = sb("badac", [128, 144]); modc = sb("modc", [128, 144])
    der = sb("der", [128, 8, 16])
    cact = sb("cact", [128, 16], BF16)
    ident_f = sb("ident_f", [128, 128]); ident_b = sb("ident_b", [128, 128], BF16)
    ones_f = sb("ones_f", [128, 128]); ones_b = sb("ones_b", [128, 128], BF16)
    R1 = sb("R1", [128, 128]); R2 = sb("R2", [128, 128]); R3 = sb("R3", [16, 128]); PI = sb("PI", [8, 128], I32)
    wsT = sb("wsT", [128, 8, 128], BF16)
    bc1 = sb("bc1", [128, 1024])
    qlat_bc = sb("qlat_bc", [128, 512]); kvlat_bc = sb("kvlat_bc", [128, 256]); krope_bc = sb("krope_bc", [128, 64])
    qn_bc = sb("qn_bc", [128, 128]); qr_bc = sb("qr_bc", [128, 64]); kn_bc = sb("kn_bc", [128, 128]); invf_bc = sb("invf_bc", [128, 32])
    cosT = sb("cosT", [128, 8, 32]); sinT = sb("sinT", [128, 8, 32]); angT = sb("angT", [128, 8, 32])
    rstd_bc = sb("rstd_bc", [128, 2, 512])
    sqr = sb("sqr", [128, 2, 512], BF16); tmpr = sb("tmpr", [128, 2, 512])
    S1 = sb("S1", [128, 1536]); S2 = sb("S2", [128, 1024])
    st = sb("st", [128, 64])
    kpe2 = sb("kpe2", [128, 128], BF16)
    qnb = sb("qnb", [128, 8, 128], BF16); qpb = sb("qpb", [128, 8, 64], BF16)
    knb = sb("knb", [128, 2, 4, 128], BF16)

    PSB = [ps(f"P{i}", [128, 512]) for i in range(5)]
    PA = ps("PA", [128, 1024])
    TRB = ps("TRB", [128, 1024], BF16)
    pb = [Buf(f"P{i}") for i in range(5)]
    pa0, pa1 = Buf("PA0"), Buf("PA1")
    trb = Buf("TRB")

    hF = A.bitcast(F32)
    xsF = NT.bitcast(F32)

    def hv(c, half):
        return hF[:, c * T + half * 512: c * T + half * 512 + 512]

    def ntv(c, half):
        return NT[:, c * T + half * 512: c * T + half * 512 + 512]

    hb = [[Buf(f"h{c}_{h}") for h in range(2)] for c in range(NCH)]
    ntb = [[Buf(f"nt{c}_{h}") for h in range(2)] for c in range(NCH)]
    actb = [[Buf(f"act{f}_{h}") for h in range(2)] for f in range(4)]
    sqb = [Buf("sq0"), Buf("sq1")]
    tmpb = [Buf("tmp0"), Buf("tmp1")]
    rsb = [Buf("rs0"), Buf("rs1")]
    ring_b = [Buf(f"ring{i}") for i in range(NS)]
    parb = Buf("params")
    colsb = Buf("cols"); modb = Buf("modc"); derb = Buf("der"); cactb = Buf("cact"); onesb = Buf("ones"); identb = Buf("ident"); wsb = Buf("wsT"); trigb = Buf("trig"); angb = Buf("ang"); r1b = Buf("r1")
    qnbb, kpe2b, qpbb = Buf("qnb"), Buf("kpe2"), Buf("qpb")
    knbb = [Buf("knb0"), Buf("knb1")]
    S1b, S2b, stb = Buf("S1"), Buf("S2"), Buf("st")

    items = []
    state = {"issued": 0, "released": [False] * 4096}

    def add_item(src, a, b):
        items.append((src, a, b))
        return len(items) - 1

    slot_owner = [None] * NS
    slot_of = {}

    def pump():
        while state["issued"] < len(items):
            k = state["issued"]
            free = [s_ for s_ in range(NS) if slot_owner[s_] is None]
            if not free:
                break
            s = free[0]
            slot_owner[s] = k
            slot_of[k] = s
            src, a, b = items[k]
            dst = RING[:, s, 0:a * b].rearrange("p (a b) -> p a b", a=a)
            P.dma("pool", (lambda e, dst=dst, src=src: e.dma_start(out=dst, in_=src)), dsem[f"ring{s}"], writes=[ring_b[s]])
            state["issued"] += 1

    def get_item(k):
        pump()
        assert state["issued"] > k, f"ring item {k} not issued (ring too small)"
        src, a, b = items[k]
        s = slot_of[k]
        return ring_b[s], RING[:, s, 0:a * b].rearrange("p (a b) -> p a b", a=a)

    def release(k):
        state["released"][k] = True
        slot_owner[slot_of[k]] = None
        pump()

    def colitem(w, c0, n):
        return add_item(w[:, c0:c0 + n].rearrange("(k p) n -> p k n", p=128), w.shape[0] // 128, n)

    def rowitem(w, r0, nrow):
        return add_item(w[r0:r0 + nrow, :].rearrange("(j p) n -> p j n", p=128), nrow // 128, w.shape[1])

    ada_items = [colitem(w_ada, 0 if DBG_SKIP_FFN else j * 256, 256) for j in range(16)]
    ffn_items = {1: [], 2: []}
    ada_rest = []

    def add_ffn_items(k):
        Wg, Wu, Wd = ffn_w[k]
        for b in range(NBLK):
            blk = {}
            for pr in range(2):
                if DBG_SKIP_FFN:
                    break
                f0 = (b * 4 + pr * 2) * 128
                blk[("g", pr)] = colitem(Wg, f0, 256)
                blk[("u", pr)] = colitem(Wu, f0, 256)
            if k == 1 and b == 0:
                blk["ada_early"] = [(j, colitem(w_ada, 0 if DBG_SKIP_FFN else j * 256, 256)) for j in range(16, 24)]
            for pr in range(2):
                if DBG_SKIP_FFN:
                    break
                f0 = (b * 4 + pr * 2) * 128
                blk[("d", pr)] = rowitem(Wd, f0, 256)
            ffn_items[k].append(blk)
            if k == 1:
                lo = 24 + (24 * b) // NBLK
                hi = 24 + (24 * (b + 1)) // NBLK
                blk["ada"] = [(j, colitem(w_ada, 0 if DBG_SKIP_FFN else j * 256, 256)) for j in range(lo, hi)]

    add_ffn_items(1)
    win_u = [colitem(w_in, i * 256, 256) for i in range(4)]
    win_v = [colitem(w_in, 1024 + i * 256, 256) for i in range(4)]
    win_cq = [colitem(w_in, 2048 + i * 256, 256) for i in range(2)]
    win_ckv = colitem(w_in, 2560, 256)
    win_kpe = colitem(w_in, 2624, 256)
    n_items_dbg = len(items)
    ada_late_a = [(j, colitem(w_ada, 0 if DBG_SKIP_FFN else j * 256, 256)) for j in range(48, 60)]
    wuq_items = [colitem(w_uq, i * 768, 768) for i in range(2)]
    ada_late_b = [(j, colitem(w_ada, 0 if DBG_SKIP_FFN else j * 256, 256)) for j in range(60, 72)]
    wukv_item = rowitem(w_ukv, 0, 256)
    wout_items = [colitem(w_out, i * 256, 256) for i in range(8)]
    add_ffn_items(2)

    if DBGX == 5:
        del items[n_items_dbg:]

    def mm(out, lhsT, rhs, start, stop, reads, writes, inc=None):
        if inc is None:
            inc = stop
        P.op("pe", (lambda e: e.matmul(out, lhsT, rhs, start=start, stop=stop)), reads=reads, writes=writes, inc=inc)

    def tr(out, in_, ident, reads, writes, inc):
        P.op("pe", (lambda e: e.transpose(out, in_, ident)), reads=reads, writes=writes, inc=inc)

    def act(out, in_, func, reads, writes, bias=None, scale=None):
        kw = {}
        if bias is not None:
            kw["bias"] = bias
        if scale is not None:
            kw["scale"] = scale
        P.op("act", (lambda e: e.activation(out=out, in_=in_, func=func, **kw)), reads=reads, writes=writes)

    def dve(fn, reads, writes):
        P.op("dve", fn, reads=reads, writes=writes)

    def tt_(out, in0, in1, op, reads, writes):
        dve((lambda e: e.tensor_tensor(out=out, in0=in0, in1=in1, op=op)), reads, writes)

    def stt(out, in0, scalar, in1, op0, op1, reads, writes):
        dve((lambda e: e.scalar_tensor_tensor(out=out, in0=in0, scalar=scalar, in1=in1, op0=op0, op1=op1)), reads, writes)

    def ts(out, in0, s1, s2, op0, op1, reads, writes):
        if s2 is None:
            dve((lambda e: e.tensor_single_scalar(out=out, in_=in0, scalar=s1, op=op0)), reads, writes)
        else:
            dve((lambda e: e.tensor_scalar(out=out, in0=in0, scalar1=s1, scalar2=s2, op0=op0, op1=op1)), reads, writes)

    def vcopy(out, in_, reads, writes):
        dve((lambda e: e.tensor_copy(out=out, in_=in_)), reads, writes)

    def rsum(out, in_, reads, writes):
        dve((lambda e: e.tensor_reduce(out=out, in_=in_, axis=AX.X, op=ALU.add)), reads, writes)

    def recip(out, in_, reads, writes):
        dve((lambda e: e.reciprocal(out=out, in_=in_)), reads, writes)

    def rstd_small(dst, ss, n, reads, writes):
        act(dst, ss, AF.Sqrt, reads, writes, bias=EPS, scale=1.0 / n)
        recip(dst, dst, writes, writes)

    par = dsem["par"]

    pl_state = [0]

    def pload(dst, src):
        q = "sp" if pl_state[0] % 2 == 0 else "act"
        pl_state[0] += 1
        P.dma(q, (lambda e: e.dma_start(out=dst, in_=src)), par)

    def rows(v):
        return v.rearrange("(c p) -> c p", p=128)

    pload(ident_f[:], ident_d)
    pload(R1[0:16, :], rows(ffn1_norm)); pload(R1[16:32, :], rows(mix_norm)); pload(R1[32:48, :], rows(ffn2_norm)); pload(R1[48:64, :], rows(final_norm))
    pload(PI[:, :], rows(pos_d))
    pload(R1[72:80, :], rows(out_norm_gmlp)); pload(R1[80:88, :], rows(out_norm_mla)); pload(R1[88:104, :], rows(c_d))
    pload(R2[:, :], rows(b_ada)[0:128, :]); pload(R3[:, :], rows(b_ada)[128:144, :])
    pload(S1[:, 0:1024].rearrange("p (h q) -> p h q", h=8), gmlp_w_s.rearrange("h p q -> p h q"))
    pload(bc1[:], gmlp_v_norm.partition_broadcast(128))
    pload(qlat_bc[:], q_lat_norm.partition_broadcast(128)); pload(kvlat_bc[:], kv_lat_norm.partition_broadcast(128))
    pload(krope_bc[:], k_rope_norm.partition_broadcast(128)); pload(qn_bc[:], q_nope_norm.partition_broadcast(128))
    pload(qr_bc[:], q_rope_norm.partition_broadcast(128)); pload(kn_bc[:], k_nope_norm.partition_broadcast(128))
    pload(invf_bc[:], invf_d.partition_broadcast(128))
    par_tok = ("par", par.n)
    parb.w = par_tok
    S1b.w = par_tok

    P.op("pool", (lambda e: e.memset(ones_f[:], 1.0)), writes=[onesb])
    P.op("pool", (lambda e: e.memset(ones_b[:], 1.0)), writes=[onesb])
    vcopy(ident_b[:], ident_f[:], [parb], [identb])
    vcopy(R1[64:72, :], PI[:, :], [parb], [r1b])
    tr(PSB[0][:, 0:104], R1[0:104, :], ident_f[0:104, 0:104], [parb, r1b, identb], [pb[0]], True)
    vcopy(cols[:], PSB[0][:, 0:104], [pb[0]], [colsb])
    tr(PSB[1][:, 0:128], R2[:, :], ident_f[:], [parb, identb], [pb[1]], False)
    tr(PSB[1][:, 128:144], R3[:, :], ident_f[0:16, 0:16], [parb, identb], [pb[1]], True)
    vcopy(badac[:], PSB[1][:, 0:144], [pb[1]], [colsb])
    for g in range(2):
        for j in range(4):
            hh = g * 4 + j
            tr(PSB[3 + g][:, j * 128:(j + 1) * 128], S1[:, hh * 128:(hh + 1) * 128], ident_f[:], [S1b, identb], [pb[3 + g]], j == 3)
        vcopy(wsT[:, g * 4:(g + 1) * 4, :], PSB[3 + g][:, :].rearrange("p (a b) -> p a b", a=4), [pb[3 + g]], [wsb])
    act(cact[:], cols[:, 88:104], AF.Silu, [colsb], [cactb])
    for t8 in range(8):
        ts(angT[:, t8, :], invf_bc[:], cols[:, 64 + t8:65 + t8], None, ALU.mult, ALU.bypass, [colsb, parb], [angb])
    TWO_PI = float(2 * np.pi)
    angf = angT[:].rearrange("p a b -> p (a b)")
    ITt = sb("ITt", [128, 256], I32)

    def sin_table(dst, src):
        ts(S2[:, 256:512], src, 1.0 / TWO_PI, None, ALU.mult, None, [S2b, angb], [S2b])
        vcopy(ITt[:, :], S2[:, 256:512], [S2b], [ittb])
        vcopy(S2[:, 256:512], ITt[:, :], [ittb], [S2b])
        stt(S2[:, 512:768], S2[:, 256:512], -TWO_PI, src, ALU.mult, ALU.add, [S2b, angb], [S2b])
        ts(S2[:, 512:768], S2[:, 512:768], -3.1415925, 3.1415925, ALU.max, ALU.min, [S2b], [S2b])
        act(dst, S2[:, 512:768], AF.Sin, [S2b], [trigb])

    ittb = Buf("itt")
    sin_table(sinT[:].rearrange("p a b -> p (a b)"), angf)
    ts(S2[:, 0:256], angf, float(np.pi / 2), None, ALU.add, None, [angb], [S2b])
    sin_table(cosT[:].rearrange("p a b -> p (a b)"), S2[:, 0:256])
    for b_ in (onesb, identb, wsb, trigb, parb, colsb):
        pass

    def ada_consume(j2, it):
        rb, view = get_item(it)
        for jj in range(2):
            j = j2 * 2 + jj
            for k in range(NCH):
                mm(PSB[2][:, j:j + 1], view[:, k, jj * 128:(jj + 1) * 128], cact[:, k:k + 1], k == 0, k == NCH - 1,
                   [rb, cactb], [pb[2]], inc=(k == NCH - 1))
        release(it)


    xsb = [Buf("xs0"), Buf("xs1")]
    bank_rot = [0, 1, 3, 4]
    ev = 0
    for t8 in range(8):
        s = t8 % 2
        xst = xsF[:, s * 2048:(s + 1) * 2048]
        P.dma("sp", (lambda e, xst=xst, t8=t8: e.dma_start(out=xst, in_=x_d[t8 * 128:(t8 + 1) * 128, :])), dsem[f"xs{s}"], writes=[xsb[s]])
        for cg in range(4):
            bk = bank_rot[(t8 * 4 + cg) % 4]
            for j in range(4):
                c = cg * 4 + j
                tr(PSB[bk][:, j * 128:(j + 1) * 128], xst[:, c * 128:(c + 1) * 128], ident_f[:], [xsb[s], identb], [pb[bk]], j == 3)
            outv = hF[:, :].rearrange("p (c t) -> p c t", c=NCH)[:, cg * 4:(cg + 1) * 4, t8 * 128:(t8 + 1) * 128]
            inv = PSB[bk][:, :].rearrange("p (a b) -> p a b", a=4)
            wr = [hb[cg * 4 + j][t8 // 4] for j in range(4)]
            if ev % 2 == 0:
                vcopy(outv, inv, [pb[bk]], wr)
            else:
                act(outv, inv, AF.Identity, [pb[bk]], wr)
            ev += 1
        ada_consume(2 * t8, ada_items[2 * t8])
        ada_consume(2 * t8 + 1, ada_items[2 * t8 + 1])

    for c_ in range(NCH):
        for h_ in range(2):
            inherit(ntb[c_][h_], xsb)

    def derive(idx, sc_lo, norm_lo):
        stt(der[:, idx, :], modc[:, sc_lo:sc_lo + 16], 1.0, cols[:, norm_lo:norm_lo + 16], ALU.add, ALU.mult, [modb, colsb], [derb])


    def norm_stats(src_fn, src_bufs, nchunks, nfeat):
        for half in range(2):
            for c in range(nchunks):
                s = c % 2
                act(sqr[:, s, :], src_fn(c, half), AF.Square, [src_bufs[c][half]], [sqb[s]])
                mm(PSB[2][:, :], ones_b[:], sqr[:, s, :], c == 0, c == nchunks - 1, [sqb[s], onesb], [pb[2]], inc=True)
            act(rstd_bc[:, half, :], PSB[2][:, :], AF.Sqrt, [pb[2]], [rsb[half]], bias=EPS, scale=1.0 / nfeat)
            recip(rstd_bc[:, half, :], rstd_bc[:, half, :], [rsb[half]], [rsb[half]])

    def norm_mod(a_cols, s_cols, do_stats=True):
        if do_stats:
            norm_stats(hv, hb, NCH, D)
        for half in range(2):
            for c in range(NCH):
                s = c % 2
                stt(tmpr[:, s, :], hv(c, half), a_cols[:, c:c + 1], rstd_bc[:, half, :], ALU.mult, ALU.mult,
                    [hb[c][half], rsb[half], derb], [tmpb[s]])
                act(ntv(c, half), tmpr[:, s, :], AF.Identity, [tmpb[s], modb], [ntb[c][half]], bias=s_cols[:, c:c + 1])

    GU = [(PSB[0], pb[0], PSB[1], pb[1]), (PSB[3], pb[3], PSB[4], pb[4])]
    DB = [(PA[:, 0:512], pa0), (PA[:, 512:1024], pa1)]
    actT = Bt[:, 0:4096].rearrange("p (f t) -> p f t", f=4)

    def ffn(k, gh_cols):
        for b in range(NBLK):
            blk = ffn_items[k][b]
            for fc in range(4):
                if DBG_SKIP_FFN:
                    break
                pr = fc // 2
                off = (fc % 2) * 128
                gb, gv = get_item(blk[("g", pr)])
                ub, uv = get_item(blk[("u", pr)])
                for half in range(2):
                    G, Gb, U, Ub = GU[(fc * 2 + half) % 2]
                    for kk in range(NCH):
                        mm(G[:, :], gv[:, kk, off:off + 128], ntv(kk, half), kk == 0, kk == NCH - 1, [gb, ntb[kk][half]], [Gb])
                    for kk in range(NCH):
                        mm(U[:, :], uv[:, kk, off:off + 128], ntv(kk, half), kk == 0, kk == NCH - 1, [ub, ntb[kk][half]], [Ub])
                    s = (fc * 2 + half) % 2
                    act(tmpr[:, s, :], G[:, :], AF.Silu, [Gb], [tmpb[s]])
                    tt_(actT[:, fc, half * 512:(half + 1) * 512], tmpr[:, s, :], U[:, :], ALU.mult, [tmpb[s], Ub], [actb[fc][half]])
                if fc % 2 == 1:
                    release(blk[("g", pr)])
                    release(blk[("u", pr)])
            if "ada_early" in blk:
                for (j2, it) in blk["ada_early"]:
                    ada_consume(j2, it)
                tt_(modc[:, 32:48], PSB[2][:, 32:48], badac[:, 32:48], ALU.add, [pb[2], colsb], [modb])
                ts(der[:, 1, :], modc[:, 32:48], 0.5, None, ALU.mult, ALU.bypass, [modb], [derb])
            if not DBG_SKIP_FFN:
                d0b, d0v = get_item(blk[("d", 0)])
                d1b, d1v = get_item(blk[("d", 1)])
                dvs = [(d0b, d0v), (d1b, d1v)]
            for dc in range(NCH):
                if DBG_SKIP_FFN:
                    break
                for half in range(2):
                    Dv, Db = DB[(dc * 2 + half) % 2]
                    for fc in range(4):
                        db_, dv_ = dvs[fc // 2]
                        mm(Dv, dv_[:, fc % 2, dc * 128:(dc + 1) * 128], actT[:, fc, half * 512:(half + 1) * 512], fc == 0, fc == 3,
                           [db_, actb[fc][half]], [Db])
                    stt(hv(dc, half), Dv, gh_cols[:, dc:dc + 1], hv(dc, half), ALU.mult, ALU.add, [Db, hb[dc][half], derb, modb], [hb[dc][half]])
            if not DBG_SKIP_FFN:
                release(blk[("d", 0)])
                release(blk[("d", 1)])
            for (j2, it) in blk.get("ada", []):
                ada_consume(j2, it)

    tt_(modc[:, 0:32], PSB[2][:, 0:32], badac[:, 0:32], ALU.add, [pb[2], colsb], [modb])
    norm_stats(hv, hb, NCH, D)
    derive(0, 16, 0)
    norm_mod(der[:, 0, :], modc[:, 0:16], do_stats=False)
    ffn(1, der[:, 1, :])
    tt_(modc[:, 48:96], PSB[2][:, 48:96], badac[:, 48:96], ALU.add, [pb[2], colsb], [modb])
    derive(2, 64, 16)

    allh = [hb[c][h] for c in range(NCH) for h in range(2)]

    if STAGE >= 2:
        norm_mod(der[:, 2, :], modc[:, 48:64])
        P.wait_all("sp", [b_.w for b_ in allh])
        for i4 in range(4):
            P.dma("sp", (lambda e, i4=i4: e.dma_start(out=hsp.ap()[:, i4 * 4096:(i4 + 1) * 4096], in_=hF[:, i4 * 4096:(i4 + 1) * 4096])), dsem["spill"])
        spill_tok = ("spill", dsem["spill"].n)
        for b_ in allh:
            b_.rs.append(spill_tok)
        uT = A[:, 0:8192].rearrange("p (c t) -> p c t", c=8)
        vN = A[:, 8192:16384].rearrange("p (t f) -> p t f", t=8)
        mT = A[:, 8192:16384].rearrange("p (c t) -> p c t", c=8)
        qTn = A[:, 16384:24576].rearrange("p (h t) -> p h t", h=8)
        qTp = A[:, 24576:28672].rearrange("p (h t) -> p h t", h=4)
        kpeA = A[:, 28672:32768]
        kvlA = Bt[:, 0:8192].rearrange("p (c t) -> p c t", c=2)
        ub = [Buf(f"uT{c}") for c in range(8)]
        vb = [Buf(f"vN{t}") for t in range(8)]
        mb = [[Buf(f"mT{h}_{q}") for q in range(2)] for h in range(8)]
        qTb = Buf("qT"); kpeAb = Buf("kpeA"); kvlAb = Buf("kvlA")
        qlb = [Buf(f"qlt{t}") for t in range(8)]
        ownb = Buf("own")
        for nb in ub + vb + [qTb, kpeAb]:
            inherit(nb, allh)
        inherit(kvlAb, [actb[f][h] for f in range(4) for h in range(2)])

        for _once in (0,):
            for i in range(4):
                rb, view = get_item(win_u[i])
                for jj in range(2):
                    uc = i * 2 + jj
                    for half in range(2):
                        bk = (uc * 2 + half) % 2
                        for kk in range(NCH):
                            mm(PSB[bk][:, :], view[:, kk, jj * 128:(jj + 1) * 128], ntv(kk, half), kk == 0, kk == NCH - 1, [rb, ntb[kk][half]], [pb[bk]])
                        act(uT[:, uc, half * 512:(half + 1) * 512], PSB[bk][:, :], AF.Gelu_apprx_tanh, [pb[bk]], [ub[uc]])
                release(win_u[i])

            if MIXCUT < 1:
                break
            vit = [get_item(i) for i in win_v]
            if DBGX == 2:
                for i in range(4):
                    rb, view = vit[i]
                    for jj in range(2):
                        for half in range(2):
                            for kk in range(NCH):
                                mm(PSB[3][:, :], view[:, kk, jj * 128:(jj + 1) * 128], ntv(kk, half), kk == 0, kk == NCH - 1, [rb, ntb[kk][half]], [pb[3]])
            for t8 in range(8 if DBGX == 0 else (1 if DBGX == 1 else 0)):
                for i in range(4):
                    rb, view = vit[i]
                    pbuf = pa0 if i < 2 else pa1
                    for kk in range(NCH):
                        if DBGH1:
                            mm(PSB[3 + i // 2][:, (i % 2) * 256:(i % 2 + 1) * 256], NT[:, kk * T + t8 * 128: kk * T + (t8 + 1) * 128], view[:, kk, :], kk == 0, kk == NCH - 1,
                               [rb, ntb[kk][t8 // 4]], [pb[3 + i // 2]])
                        else:
                            mm(PA[:, i * 256:(i + 1) * 256], NT[:, kk * T + t8 * 128: kk * T + (t8 + 1) * 128], view[:, kk, :], kk == 0, kk == NCH - 1,
                               [rb, ntb[kk][t8 // 4]], [pbuf])
                if DBGSUB >= 1:
                    act(S2[:, 0:512], PA[:, 0:512], AF.Gelu_apprx_tanh, [pa0], [S2b])
                    act(S2[:, 512:1024], PA[:, 512:1024], AF.Gelu_apprx_tanh, [pa1], [S2b])
                if DBGSUB >= 2:
                    tt_(S1[:, 0:1024], S2[:, :], S2[:, :], ALU.mult, [S2b], [S1b])
                    rsum(st[:, 0:1], S1[:, 0:1024], [S1b], [stb])
                if DBGSUB >= 3:
                    rstd_small(st[:, 1:2], st[:, 0:1], 1024.0, [stb], [stb])
                if DBGSUB >= 4:
                    stt(vN[:, t8, :], S2[:, :], st[:, 1:2], bc1[:, :], ALU.mult, ALU.mult, [S2b, stb, parb], [vb[t8]])
            for i in win_v:
                release(i)
            bc1b = Buf("bc1")
            inherit(bc1b, vb)
            if DBGSUB >= 5:
                P.dma("sp", (lambda e: e.dma_start(out=bc1[:], in_=gmlp_b_s.partition_broadcast(128))), dsem["reload"], writes=[bc1b])

            if MIXCUT < 2:
                break
            cq0 = get_item(win_cq[0]); cq1 = get_item(win_cq[1]); ckv = get_item(win_ckv); kpe = get_item(win_kpe)
            kvown = OWN[:, 0:2, :]
            kpeown = OWN[:, 2, :]
            def m1c_mm(t8):
                    tok = slice(t8 * 128, (t8 + 1) * 128)
                    for (rb, view), (dst, pbuf) in (((cq0), (PSB[3][:, 0:256], pb[3])), ((cq1), (PSB[3][:, 256:512], pb[3])),
                                                      ((ckv), (PSB[4][:, 0:256], pb[4])), ((kpe[0], kpe[1][:, :, 192:256]), (PSB[4][:, 256:320], pb[4]))):
                        for kk in range(NCH):
                            mm(dst, NT[:, kk * T + t8 * 128: kk * T + (t8 + 1) * 128], view[:, kk, :], kk == 0, kk == NCH - 1,
                               [rb, ntb[kk][t8 // 4]], [pbuf])
            def m1c_chain(t8):
                    tok = slice(t8 * 128, (t8 + 1) * 128)
                    act(S1[:, 0:512], PSB[3][:, :], AF.Identity, [pb[3]], [S1b])
                    act(S1[:, 512:832], PSB[4][:, 0:320], AF.Identity, [pb[4]], [S1b])
                    tt_(S2[:, 0:832], S1[:, 0:832], S1[:, 0:832], ALU.mult, [S1b], [S2b])
                    rsum(st[:, 2:3], S2[:, 0:512], [S2b], [stb])
                    rsum(st[:, 3:4], S2[:, 512:768], [S2b], [stb])
                    rsum(st[:, 4:5], S2[:, 768:832], [S2b], [stb])
                    rstd_small(st[:, 5:6], st[:, 2:3], 512.0, [stb], [stb])
                    rstd_small(st[:, 6:7], st[:, 3:4], 256.0, [stb], [stb])
                    rstd_small(st[:, 7:8], st[:, 4:5], 64.0, [stb], [stb])
                    qstage = qnb[:, 0:4, :]
                    stt(qstage.rearrange("p a b -> p (a b)"), S1[:, 0:512], st[:, 5:6], qlat_bc[:, :], ALU.mult, ALU.mult, [S1b, stb, parb], [qnbb])
                    kvstage = qnb[:, 4:6, :]
                    stt(kvstage.rearrange("p a b -> p (a b)"), S1[:, 512:768], st[:, 6:7], kvlat_bc[:, :], ALU.mult, ALU.mult, [S1b, stb, parb], [qnbb])
                    stt(S2[:, 0:64], S1[:, 768:832], st[:, 7:8], krope_bc[:, :], ALU.mult, ALU.mult, [S1b, stb, parb], [S2b])
                    x1 = S2[:, 0:32]; x2 = S2[:, 32:64]
                    tt_(S2[:, 64:96], x1, cosT[:, t8, :], ALU.mult, [S2b, trigb], [S2b])
                    tt_(S2[:, 96:128], x2, sinT[:, t8, :], ALU.mult, [S2b, trigb], [S2b])
                    tt_(S2[:, 128:160], x1, sinT[:, t8, :], ALU.mult, [S2b, trigb], [S2b])
                    tt_(S2[:, 160:192], x2, cosT[:, t8, :], ALU.mult, [S2b, trigb], [S2b])
                    tt_(kpe2[:, 0:32], S2[:, 64:96], S2[:, 96:128], ALU.subtract, [S2b], [kpe2b])
                    tt_(kpe2[:, 32:64], S2[:, 128:160], S2[:, 160:192], ALU.add, [S2b], [kpe2b])
                    vcopy(kpe2[:, 64:128], kpe2[:, 0:64], [kpe2b], [kpe2b])
            def m1c_tr(t8):
                    tok = slice(t8 * 128, (t8 + 1) * 128)
                    for j in range(4):
                        tr(TRB[:, j * 128:(j + 1) * 128], qnb[:, j, :], ident_b[:], [qnbb, identb], [trb], False)
                    for j in range(2):
                        tr(TRB[:, (4 + j) * 128:(5 + j) * 128], qnb[:, 4 + j, :], ident_b[:], [qnbb, identb], [trb], False)
                    tr(TRB[:, 6 * 128:7 * 128], kpe2[:, :], ident_b[:], [kpe2b, identb], [trb], True)
                    vcopy(QLT[:, :, tok], TRB[:, 0:512].rearrange("p (a b) -> p a b", a=4), [trb], [qlb[t8]])
                    vcopy(OWN[:, :, tok], TRB[:, 512:896].rearrange("p (a b) -> p a b", a=3), [trb], [ownb])

            for t8 in range(8):
                m1c_mm(t8)
                if t8 > 0:
                    m1c_tr(t8 - 1)
                m1c_chain(t8)
            m1c_tr(7)
            for i in win_cq + [win_ckv, win_kpe]:
                release(i)

            if MIXCUT < 3:
                break
            P.dma("sp", (lambda e: e.dma_start(out=gin.ap().rearrange("(c p) t -> p c t", p=128), in_=OWN[:, :, :])), dsem["gin"], reads=[ownb])
            gin_tok = ("gin", dsem["gin"].n)
            ccb = Buf("cc")
            P.dma("pool", (lambda e: e.collective_compute("AllGather", ALU.bypass, replica_groups=[[0, 1, 2, 3], [4, 5, 6, 7]],
                                                          ins=[gin.ap().opt()], outs=[gout.ap().opt()])), dsem["cc"], writes=[ccb], extra=[gin_tok], amt=1)
            if MIXCUT < 4:
                break
            for t8 in range(8):
                for (j2_, it_) in ada_late_a[(12 * t8) // 8:(12 * (t8 + 1)) // 8]:
                    ada_consume(j2_, it_)
                tok = slice(t8 * 128, (t8 + 1) * 128)
                for g in range(2):
                    bk = g
                    for j in range(4):
                        hg = g * 4 + j
                        mm(PSB[bk][:, j * 128:(j + 1) * 128], vN[:, t8, hg * 128:(hg + 1) * 128], wsT[:, hg, :], True, True, [vb[t8], wsb], [pb[bk]], inc=(j == 3))
                    s2v = S2[:, g * 512:(g + 1) * 512].rearrange("p (a b) -> p a b", a=4)
                    tt_(s2v, PSB[bk][:, :].rearrange("p (a b) -> p a b", a=4), bc1[:, g * 512:(g + 1) * 512].rearrange("p (a b) -> p a b", a=4), ALU.add,
                        [pb[bk], bc1b], [S2b])
                    uv_ = uT[:, g * 4:(g + 1) * 4, tok]
                    tt_(uv_, s2v, uv_, ALU.mult, [S2b] + ub[g * 4:(g + 1) * 4], ub[g * 4:(g + 1) * 4])

            if MIXCUT < 5:
                break
            q0b, q0v = get_item(wuq_items[0]); q1b, q1v = get_item(wuq_items[1])
            for t8 in range(8):
                for (j2_, it_) in ada_late_b[(12 * t8) // 8:(12 * (t8 + 1)) // 8]:
                    ada_consume(j2_, it_)
                tok = slice(t8 * 128, (t8 + 1) * 128)
                for n3 in range(3):
                    rb, view = (q0b, q0v) if n3 < 2 else (q1b, q1v)
                    if n3 == 0:
                        pieces = [(q0b, q0v[:, :, 0:512], PSB[0][:, :])]
                    elif n3 == 1:
                        pieces = [(q0b, q0v[:, :, 512:768], PSB[1][:, 0:256]), (q1b, q1v[:, :, 0:256], PSB[1][:, 256:512])]
                    else:
                        pieces = [(q1b, q1v[:, :, 256:768], PSB[3][:, :])]
                    bkb = [pb[0], pb[1], pb[3]][n3]
                    for (rb_, wv, dst) in pieces:
                        for kk in range(4):
                            mm(dst, QLT[:, kk, tok], wv[:, kk, :], kk == 0, kk == 3, [rb_, qlb[t8]], [bkb])
                act(S1[:, 0:512], PSB[0][:, :], AF.Identity, [pb[0]], [S1b])
                act(S1[:, 512:1024], PSB[1][:, :], AF.Identity, [pb[1]], [S1b])
                act(S1[:, 1024:1536], PSB[3][:, :], AF.Identity, [pb[3]], [S1b])
                q3 = S1[:, :].rearrange("p (h d) -> p h d", h=8)
                sq3 = S2[:, 0:1024].rearrange("p (h d) -> p h d", h=8)
                tt_(sq3, q3[:, :, 0:128], q3[:, :, 0:128], ALU.mult, [S1b], [S2b])
                rsum(st[:, 8:16], sq3, [S2b], [stb])
                rstd_small(st[:, 16:24], st[:, 8:16], 128.0, [stb], [stb])
                tt_(sq3, q3[:, :, 0:128], st[:, 16:24].unsqueeze(2).to_broadcast([128, 8, 128]), ALU.mult, [S1b, stb, S2b], [S2b])
                tt_(qnb[:, :, :], sq3, qn_bc[:, :].unsqueeze(1).to_broadcast([128, 8, 128]), ALU.mult, [S2b, parb], [qnbb])
                sp3 = S2[:, 0:512].rearrange("p (h d) -> p h d", h=8)
                tt_(sp3, q3[:, :, 128:192], q3[:, :, 128:192], ALU.mult, [S1b, S2b], [S2b])
                rsum(st[:, 24:32], sp3, [S2b], [stb])
                rstd_small(st[:, 32:40], st[:, 24:32], 64.0, [stb], [stb])
                tt_(sp3, q3[:, :, 128:192], st[:, 32:40].unsqueeze(2).to_broadcast([128, 8, 64]), ALU.mult, [S1b, stb, S2b], [S2b])
                tt_(sp3, sp3, qr_bc[:, :].unsqueeze(1).to_broadcast([128, 8, 64]), ALU.mult, [S2b, parb], [S2b])
                cb = cosT[:, t8, :].unsqueeze(1).to_broadcast([128, 8, 32])
                sbb = sinT[:, t8, :].unsqueeze(1).to_broadcast([128, 8, 32])
                r4 = tmpr[:, :, :].rearrange("p k (h d) -> p k h d", h=8)
                tt_(r4[:, 0, :, 0:32], sp3[:, :, 0:32], cb, ALU.mult, [S2b, trigb], tmpb)
                tt_(r4[:, 0, :, 32:64], sp3[:, :, 32:64], sbb, ALU.mult, [S2b, trigb], tmpb)
                tt_(r4[:, 1, :, 0:32], sp3[:, :, 0:32], sbb, ALU.mult, [S2b, trigb], tmpb)
                tt_(r4[:, 1, :, 32:64], sp3[:, :, 32:64], cb, ALU.mult, [S2b, trigb], tmpb)
                tt_(qpb[:, :, 0:32], r4[:, 0, :, 0:32], r4[:, 0, :, 32:64], ALU.subtract, tmpb, [qpbb])
                tt_(qpb[:, :, 32:64], r4[:, 1, :, 0:32], r4[:, 1, :, 32:64], ALU.add, tmpb, [qpbb])
                for hh in range(8):
                    tr(TRB[:, hh * 128:(hh + 1) * 128], qnb[:, hh, :], ident_b[:], [qnbb, identb], [trb], hh == 7)
                vcopy(qTn[:, :, tok], TRB[:, :].rearrange("p (a b) -> p a b", a=8), [trb], [qTb])
                qp2 = qpb[:, :, :].rearrange("p (a b) d -> p a (b d)", a=4)
                for pp in range(4):
                    tr(TRB[:, pp * 128:(pp + 1) * 128], qp2[:, pp, :], ident_b[:], [qpbb, identb], [trb], pp == 3)
                vcopy(qTp[:, :, tok], TRB[:, 0:512].rearrange("p (a b) -> p a b", a=4), [trb], [qTb])
            release(wuq_items[0]); release(wuq_items[1])
            tt_(modc[:, 96:144], PSB[2][:, 96:144], badac[:, 96:144], ALU.add, [pb[2], colsb], [modb])
            derive(3, 112, 32)
            ts(der[:, 4, :], modc[:, 128:144], 0.5, None, ALU.mult, ALU.bypass, [modb], [derb])

            if MIXCUT < 6:
                break
            inherit(kvlAb, [actb[f][h] for f in range(4) for h in range(2)] + qlb)
            P.wait_all("sp", [ccb.w] + kvlAb.rs + kpeAb.rs)
            for r in range(4):
                src_kv = gout.ap()[r * 384:r * 384 + 256, :].rearrange("(c p) t -> p c t", p=128)
                P.dma("sp", (lambda e, r=r, src_kv=src_kv: e.dma_start(out=kvlA[:, :, r * T:(r + 1) * T], in_=src_kv)), dsem["gout"])
                src_pe = gout.ap()[r * 384 + 256:r * 384 + 384, :]
                P.dma("sp", (lambda e, r=r, src_pe=src_pe: e.dma_start(out=kpeA[:, r * T:(r + 1) * T], in_=src_pe)), dsem["gout"])
            gout_tok = ("gout", dsem["gout"].n)
            kvlAb.w = gout_tok; kvlAb.rs = []
            kpeAb.w = gout_tok; kpeAb.rs = []

            kvb_, kvv = get_item(wukv_item)
            KT = [NT[:, s * 8192: s * 8192 + 4096] for s in range(2)]
            VV = [NT[:, s * 8192 + 4096: s * 8192 + 8192].rearrange("p (t d) -> p t d", t=32) for s in range(2)]
            ktb = [[Buf(f"KT{s}_{j}") for j in range(8)] for s in range(2)]
            vvb = [[Buf(f"VV{s}_{j}") for j in range(8)] for s in range(2)]
            allnt = [ntb[c][h] for c in range(NCH) for h in range(2)]
            for s in range(2):
                for j in range(8):
                    inherit(ktb[s][j], allnt)
                    inherit(vvb[s][j], allnt)
            prb = [Buf(f"pr{i}") for i in range(3)]
            for b_ in prb:
                inherit(b_, [ownb])
            for h8 in range(8):
                for q in range(2):
                    inherit(mb[h8][q], vb)
            SB_ = [(PSB[0], pb[0]), (PSB[1], pb[1]), (PSB[2], pb[2])]
            OB, LB = (PSB[3], pb[3]), (PSB[4], pb[4])

            def produce(h8, j):
                s = h8 % 2
                for i in range(4):
                    tile_ = j * 4 + i
                    for kk in range(2):
                        mm(PA[:, i * 256:(i + 1) * 256], kvlA[:, kk, tile_ * 128:(tile_ + 1) * 128], kvv[:, kk, h8 * 256:(h8 + 1) * 256], kk == 0, kk == 1,
                           [kvlAb, kvb_], [pa0 if i < 2 else pa1], inc=(kk == 1))
                ks = S1[:, 0:512].rearrange("p (a b) -> p a b", a=4)
                for hb_ in range(2):
                    kv2 = PA[:, hb_ * 512:(hb_ + 1) * 512].rearrange("p (a b) -> p a b", a=2)
                    pbk = pa0 if hb_ == 0 else pa1
                    act(ks[:, hb_ * 2:(hb_ + 1) * 2, :], kv2[:, :, 0:128], AF.Identity, [pbk], [S1b])
                    act(VV[s][:, j * 4 + hb_ * 2: j * 4 + hb_ * 2 + 2, :], kv2[:, :, 128:256], AF.Identity, [pbk], [vvb[s][j]])
                sq4 = S2[:, 0:512].rearrange("p (a b) -> p a b", a=4)
                tt_(sq4, ks, ks, ALU.mult, [S1b], [S2b])
                rsum(st[:, 40:44], sq4, [S2b], [stb])
                act(st[:, 44:48], st[:, 40:44], AF.Ln, [stb], [stb], bias=EPS, scale=1.0 / 128.0)
                act(st[:, 44:48], st[:, 44:48], AF.Exp, [stb], [stb], scale=-0.5)
                tt_(sq4, ks, st[:, 44:48].unsqueeze(2).to_broadcast([128, 4, 128]), ALU.mult, [S1b, stb, S2b], [S2b])
                tt_(knb[:, j % 2, :, :], sq4, kn_bc[:, :].unsqueeze(1).to_broadcast([128, 4, 128]), ALU.mult, [S2b, parb], [knbb[j % 2]])

            def produce_b(h8, j):
                s = h8 % 2
                for i in range(4):
                    tr(TRB[:, i * 128:(i + 1) * 128], knb[:, j % 2, i, :], ident_b[:], [knbb[j % 2], identb], [trb], i == 3)
                vcopy(KT[s][:, j * 512:(j + 1) * 512], TRB[:, 0:512], [trb], [ktb[s][j]])

            def s_mm(h8, q, kt):
                s = h8 % 2
                Sv, Sb = SB_[kt % 3]
                qs_ = slice(q * 512, (q + 1) * 512)
                mm(Sv[:, :], KT[s][:, kt * 128:(kt + 1) * 128], qTn[:, h8, qs_], True, False, [ktb[s][kt // 4], qTb], [Sb], inc=False)
                po = (h8 % 2) * 64
                mm(Sv[:, :], kpeA[po:po + 64, kt * 128:(kt + 1) * 128], qTp[po:po + 64, h8 // 2, qs_], False, True, [kpeAb, qTb], [Sb], inc=True)

            def pv_mm(h8, q, kt):
                s = h8 % 2
                Sv, Sb = SB_[kt % 3]
                pi = kt % 3
                act(PR[:, pi, :], Sv[:, :], AF.Exp, [Sb], [prb[pi]], scale=SM_SCALE)
                mm(OB[0][:, :], VV[s][:, kt, :], PR[:, pi, :], kt == 0, kt == 31, [vvb[s][kt // 4], prb[pi]], [OB[1]], inc=False)
                mm(LB[0][:, :], ones_b[:], PR[:, pi, :], kt == 0, kt == 31, [prb[pi], onesb], [LB[1]], inc=True)

            for j in range(8):
                produce(0, j)
                if j > 0:
                    produce_b(0, j - 1)
            produce_b(0, 7)
            for h8 in range(8):
                it = 0
                for q in range(2):
                    s_mm(h8, q, 0)
                    s_mm(h8, q, 1)
                    for kt in range(32):
                        if kt + 2 < 32:
                            s_mm(h8, q, kt + 2)
                        pv_mm(h8, q, kt)
                        it += 1
                        if h8 + 1 < 8 and it % 8 == 2:
                            produce(h8 + 1, it // 8)
                            if it // 8 > 0:
                                produce_b(h8 + 1, it // 8 - 1)
                    qs_ = slice(q * 512, (q + 1) * 512)
                    if q == 1 and h8 + 1 < 8:
                        produce_b(h8 + 1, 7)
                    recip(tmpr[:, 0, :], LB[0][:, :], [LB[1]], [tmpb[0]])
                    tt_(mT[:, h8, qs_], OB[0][:, :], tmpr[:, 0, :], ALU.mult, [OB[1], tmpb[0]], [mb[h8][q]])
            release(wukv_item)

            if MIXCUT < 7:
                break
            for c in range(NCH):
                for h in range(2):
                    inherit(ntb[c][h], [b_ for s in range(2) for b_ in ktb[s] + vvb[s]])
            ubh = [[ub[c], ub[c]] for c in range(8)]
            norm_stats(lambda c, half: uT[:, c, half * 512:(half + 1) * 512], ubh, 8, 1024.0)
            for half in range(2):
                for c in range(8):
                    stt(ntv(c, half), uT[:, c, half * 512:(half + 1) * 512], cols[:, 72 + c:73 + c], rstd_bc[:, half, :], ALU.mult, ALU.mult,
                        [ub[c], rsb[half], colsb], [ntb[c][half]])
            norm_stats(lambda c, half: mT[:, c, half * 512:(half + 1) * 512], mb, 8, 1024.0)
            for half in range(2):
                for c in range(8):
                    stt(ntv(8 + c, half), mT[:, c, half * 512:(half + 1) * 512], cols[:, 80 + c:81 + c], rstd_bc[:, half, :], ALU.mult, ALU.mult,
                        [mb[c][half], rsb[half], colsb], [ntb[8 + c][half]])
        mixbufs = ub + vb + [qTb, kpeAb] + [mb[h8][q] for h8 in range(8) for q in range(2)]
        for b_ in allh:
            inherit(b_, mixbufs)
        hb0 = Buf("hreload")
        inherit(hb0, mixbufs)
        P.wait_all("sp", [spill_tok] + hb0.rs)
        for i4 in range(4):
            P.dma("sp", (lambda e, i4=i4: e.dma_start(out=hF[:, i4 * 4096:(i4 + 1) * 4096], in_=hsp.ap()[:, i4 * 4096:(i4 + 1) * 4096])), dsem["spill"])
        rel_tok = ("spill", dsem["spill"].n)
        for b_ in allh:
            b_.w = rel_tok
            b_.rs = []
        for f in range(4):
            for h in range(2):
                inherit(actb[f][h], [kvlAb])
        if MIXCUT >= 8:
            for i in range(8):
                rb, view = get_item(wout_items[i])
                for jj in range(2):
                    dc = i * 2 + jj
                    for half in range(2):
                        bk = (dc * 2 + half) % 2
                        for kk in range(NCH):
                            mm(PSB[bk][:, :], view[:, kk, jj * 128:(jj + 1) * 128], ntv(kk, half), kk == 0, kk == NCH - 1, [rb, ntb[kk][half]], [pb[bk]])
                        stt(hv(dc, half), PSB[bk][:, :], modc[:, 80 + dc:81 + dc], hv(dc, half), ALU.mult, ALU.add, [pb[bk], hb[dc][half], modb], [hb[dc][half]])
                release(wout_items[i])

    if STAGE >= 3:
        norm_mod(der[:, 3, :], modc[:, 96:112])
        ffn(2, der[:, 4, :])

    norm_stats(hv, hb, NCH, D)
    for half in range(2):
        for c in range(NCH):
            stt(hv(c, half), hv(c, half), cols[:, 48 + c:49 + c], rstd_bc[:, half, :], ALU.mult, ALU.mult, [hb[c][half], rsb[half], colsb], [hb[c][half]])
    osb = [Buf("os0"), Buf("os1")]
    for b_ in osb:
        inherit(b_, [ntb[c][h] for c in range(NCH) for h in range(2)])
    ev = 0
    out_toks = []
    for t8 in range(8):
        s = t8 % 2
        ost = xsF[:, s * 2048:(s + 1) * 2048]
        for cg in range(4):
            bk = bank_rot[(t8 * 4 + cg) % 4]
            for j in range(4):
                c = cg * 4 + j
                tr(PSB[bk][:, j * 128:(j + 1) * 128], hF[:, c * T + t8 * 128: c * T + (t8 + 1) * 128], ident_f[:], [hb[c][t8 // 4], identb], [pb[bk]], j == 3)
            if ev % 2 == 0:
                vcopy(ost[:, cg * 512:(cg + 1) * 512], PSB[bk][:, :], [pb[bk]], [osb[s]])
            else:
                act(ost[:, cg * 512:(cg + 1) * 512], PSB[bk][:, :], AF.Identity, [pb[bk]], [osb[s]])
            ev += 1
        tk = P.dma("sp", (lambda e, ost=ost, t8=t8: e.dma_start(out=y_d[t8 * 128:(t8 + 1) * 128, :], in_=ost)), dsem[f"os{s}"], reads=[osb[s]])
        out_toks.append(tk)
    P.wait_all("sp", [("os0", dsem["os0"].n), ("os1", dsem["os1"].n)])

    engmap = {"pe": "tensor", "act": "scalar", "dve": "vector", "pool": "gpsimd", "sp": "sync"}
    with nc.Block() as block:
        def make(ename):
            def body(eng):
                for (ws, fn, inc) in P.ops[ename]:
                    for (sk, val) in ws:
                        eng.wait_ge(semh[sk], val)
                    if fn is None:
                        continue
                    ins = fn(eng)
                    if inc is not None:
                        ins.then_inc(semh[inc[0]], inc[1])
            return body
        for ename in Prog.ENG:
            getattr(block, engmap[ename])(make(ename))
    es.close()
    _CACHE['P'] = P
    return nc


_CACHE = {}


def kernel(**inputs):
    inp = {k: np.asarray(v) for k, v in inputs.items()}
    x = inp["x"]; c = inp["c"]; pos = inp["positions"]
    if "nc" not in _CACHE:
        _CACHE["nc"] = build_program()
    nc = _CACHE["nc"]
    shared = {}
    for k, v in inp.items():
        if k in ("x", "c", "positions"):
            continue
        a = np.ascontiguousarray(v[0])
        if k == "gmlp_b_s":
            a = a.reshape(-1)
        if DBG_SKIP_FFN and k == "w_ada":
            a = np.ascontiguousarray(a[:, :256])
        if DBG_SKIP_FFN and k.startswith("ffn") and "_w_" in k:
            a = np.ascontiguousarray(a[:128, :256])
        shared[k] = a
    shared["ident"] = np.eye(128, dtype=np.float32)
    shared["inv_freq"] = (10000.0 ** (-np.arange(0, 64, 2, dtype=np.float32) / 64)).astype(np.float32)
    in_maps = []
    for core in range(8):
        b, q = core // 4, core % 4
        m = dict(shared)
        m["x"] = np.ascontiguousarray(x[b, q * T:(q + 1) * T, :])
        m["c"] = np.ascontiguousarray(c[b])
        m["positions"] = np.ascontiguousarray(pos[b, q * T:(q + 1) * T]).astype(np.int32)
        in_maps.append(m)
    res = run_bass_kernel_spmd(nc, in_maps, core_ids=list(range(8)))
    out = np.empty((2, 4096, D), dtype=np.float32)
    for core in range(8):
        b, q = core // 4, core % 4
        out[b, q * T:(q + 1) * T, :] = np.asarray(res.results[core]["y"])
    return out
```
